# Optimizing a Trainium2 kernel written in Bass

```python
import math
import jax, jax.numpy as jnp
from jax import lax
import numpy as np

D_MODEL = 1024
BATCH = 4
SEQ = 8192
DEPTH = 1

D_FF = 2816
N_POOL_GROUPS = 4
POOL_WINDOWS = (2, 4, 8, 16)
POOL_GROUP_DIM = 128
D_POOL = N_POOL_GROUPS * POOL_GROUP_DIM
N_DIFF_HEADS = 8
HEAD_DIM = 64
V_DIM = 2 * HEAD_DIM
D_ATTN = N_DIFF_HEADS * V_DIM
Q_BLOCK = 128
LN_EPS = 1e-5
RMS_EPS = 1e-5
DEEPNORM_ALPHA = (2.0 * DEPTH) ** 0.25
DEEPNORM_BETA = (8.0 * DEPTH) ** (-0.25)
IN_SPLITS = (D_POOL, 2 * N_DIFF_HEADS * HEAD_DIM, 2 * N_DIFF_HEADS * HEAD_DIM, D_ATTN, D_MODEL, D_MODEL)
D_IN = sum(IN_SPLITS)

kernel_name = "hybrid_pool_diffattn_macaron_deepnorm"


def layer_norm(x, g, b):
    xf = x.astype(jnp.float32)
    mu = jnp.mean(xf, axis=-1, keepdims=True)
    var = jnp.mean(jnp.square(xf - mu), axis=-1, keepdims=True)
    y = (xf - mu) * lax.rsqrt(var + LN_EPS)
    return (y * g.astype(jnp.float32) + b.astype(jnp.float32)).astype(x.dtype)


def swiglu(x, w_gate, w_up, w_down):
    return (jax.nn.silu(x @ w_gate) * (x @ w_up)) @ w_down


def alibi_slopes(n_heads):
    return jnp.exp2(-8.0 / n_heads * jnp.arange(1, n_heads + 1, dtype=jnp.float32))


def pool_mixer(u, w_grp, b_grp, scale):
    bsz, s = u.shape[0], u.shape[1]
    ug = u.reshape(bsz, s, N_POOL_GROUPS, POOL_GROUP_DIM).astype(jnp.float32)
    c = jnp.pad(jnp.cumsum(ug, axis=1), ((0, 0), (1, 0), (0, 0), (0, 0)))
    t = jnp.arange(s)
    pooled = []
    for g, w in enumerate(POOL_WINDOWS):
        start = jnp.maximum(t + 1 - w, 0)
        win_sum = c[:, 1:, g] - c[:, start, g]
        cnt = (t + 1 - start).astype(jnp.float32)
        pooled.append(win_sum / cnt[None, :, None])
    z = (jnp.stack(pooled, axis=2) - ug).astype(u.dtype)
    z = jnp.einsum('bsgc,gcd->bsgd', z, w_grp) + b_grp
    return z.reshape(bsz, s, D_POOL) * scale


def diff_attention(q, k, v, lam, slopes):
    bsz, s = q.shape[0], q.shape[1]
    nb = s // Q_BLOCK
    q = q * (HEAD_DIM ** -0.5)
    qb = q.reshape(bsz, nb, Q_BLOCK, 2, N_DIFF_HEADS, HEAD_DIM).transpose(1, 0, 3, 4, 2, 5)
    kt = k.transpose(0, 2, 3, 1, 4)
    vt = v.transpose(0, 2, 1, 3)
    key_pos = jnp.arange(s)

    def block(args):
        qblk, i = args
        q_pos = i * Q_BLOCK + jnp.arange(Q_BLOCK)
        dist = (q_pos[:, None] - key_pos[None, :]).astype(jnp.float32)
        scores = jnp.einsum('bmhqd,bmhkd->bmhqk', qblk, kt).astype(jnp.float32)
        scores = scores - slopes[:, None, None] * dist
        scores = jnp.where(dist >= 0, scores, -jnp.inf)
        p = jax.nn.softmax(scores, axis=-1)
        a = p[:, 0] - lam * p[:, 1]
        return jnp.einsum('bhqk,bhkd->bhqd', a.astype(vt.dtype), vt)

    o = lax.map(block, (qb, jnp.arange(nb)))
    return o.transpose(1, 0, 3, 2, 4).reshape(bsz, s, N_DIFF_HEADS, V_DIM)


def setup_inputs(seed: int = 0) -> dict:
    key = jax.random.key(seed)
    ks = jax.random.split(key, 32)
    f32 = jnp.float32
    L = DEPTH

    def nrm(k, shape, scale):
        return jax.random.normal(k, shape, f32) * scale

    def gain(k, n):
        return jnp.ones((L, n), f32) + nrm(k, (L, n), 0.02)

    return {
        "x": nrm(ks[0], (BATCH, SEQ, D_MODEL), 1.0),
        "ffn1_w_gate": nrm(ks[1], (L, D_MODEL, D_FF), D_MODEL ** -0.5),
        "ffn1_w_up": nrm(ks[2], (L, D_MODEL, D_FF), D_MODEL ** -0.5),
        "ffn1_w_down": nrm(ks[3], (L, D_FF, D_MODEL), D_FF ** -0.5 * DEEPNORM_BETA),
        "ln1_g": gain(ks[4], D_MODEL),
        "ln1_b": nrm(ks[5], (L, D_MODEL), 0.02),
        "w_in": nrm(ks[6], (L, D_MODEL, D_IN), D_MODEL ** -0.5),
        "b_gate": nrm(ks[7], (L, 2 * D_MODEL), 0.02),
        "pool_w": nrm(ks[8], (L, N_POOL_GROUPS, POOL_GROUP_DIM, POOL_GROUP_DIM), POOL_GROUP_DIM ** -0.5),
        "pool_b": nrm(ks[9], (L, N_POOL_GROUPS, POOL_GROUP_DIM), 0.02),
        "pool_scale": gain(ks[10], D_POOL),
        "lambda_q1": nrm(ks[11], (L, HEAD_DIM), 0.1),
        "lambda_k1": nrm(ks[12], (L, HEAD_DIM), 0.1),
        "lambda_q2": nrm(ks[13], (L, HEAD_DIM), 0.1),
        "lambda_k2": nrm(ks[14], (L, HEAD_DIM), 0.1),
        "subln_g": gain(ks[15], V_DIM),
        "w_proj_pool": nrm(ks[16], (L, D_POOL, D_MODEL), D_POOL ** -0.5),
        "w_proj_attn": nrm(ks[17], (L, D_ATTN, D_MODEL), D_ATTN ** -0.5),
        "w_out": nrm(ks[18], (L, D_MODEL, D_MODEL), D_MODEL ** -0.5 * DEEPNORM_BETA),
        "ln2_g": gain(ks[19], D_MODEL),
        "ln2_b": nrm(ks[20], (L, D_MODEL), 0.02),
        "ffn2_w_gate": nrm(ks[21], (L, D_MODEL, D_FF), D_MODEL ** -0.5),
        "ffn2_w_up": nrm(ks[22], (L, D_MODEL, D_FF), D_MODEL ** -0.5),
        "ffn2_w_down": nrm(ks[23], (L, D_FF, D_MODEL), D_FF ** -0.5 * DEEPNORM_BETA),
        "ln3_g": gain(ks[24], D_MODEL),
        "ln3_b": nrm(ks[25], (L, D_MODEL), 0.02),
    }


def reference(x, ffn1_w_gate, ffn1_w_up, ffn1_w_down, ln1_g, ln1_b, w_in, b_gate,
              pool_w, pool_b, pool_scale, lambda_q1, lambda_k1, lambda_q2, lambda_k2,
              subln_g, w_proj_pool, w_proj_attn, w_out, ln2_g, ln2_b,
              ffn2_w_gate, ffn2_w_up, ffn2_w_down, ln3_g, ln3_b):
    bsz, s = x.shape[0], x.shape[1]
    slopes = alibi_slopes(N_DIFF_HEADS)
    split_idx = list(np.cumsum(IN_SPLITS)[:-1])
    for l in range(DEPTH):
        x = layer_norm(DEEPNORM_ALPHA * x + 0.5 * swiglu(x, ffn1_w_gate[l], ffn1_w_up[l], ffn1_w_down[l]),
                       ln1_g[l], ln1_b[l])

        h = x @ w_in[l]
        u_pool, q, k, v, g_a, g_b = jnp.split(h, split_idx, axis=-1)
        gates = jax.nn.sigmoid(jnp.concatenate([g_a, g_b], axis=-1) + b_gate[l])
        g_a, g_b = gates[..., :D_MODEL], gates[..., D_MODEL:]

        y_pool = pool_mixer(u_pool, pool_w[l], pool_b[l], pool_scale[l]) @ w_proj_pool[l]

        lam_init = 0.8 - 0.6 * math.exp(-0.3 * l)
        lam = (jnp.exp(jnp.sum(lambda_q1[l].astype(jnp.float32) * lambda_k1[l].astype(jnp.float32)))
               - jnp.exp(jnp.sum(lambda_q2[l].astype(jnp.float32) * lambda_k2[l].astype(jnp.float32)))
               + lam_init)
        q = q.reshape(bsz, s, 2, N_DIFF_HEADS, HEAD_DIM)
        k = k.reshape(bsz, s, 2, N_DIFF_HEADS, HEAD_DIM)
        v = v.reshape(bsz, s, N_DIFF_HEADS, V_DIM)
        o = diff_attention(q, k, v, lam, slopes)
        of = o.astype(jnp.float32)
        of = of * lax.rsqrt(jnp.mean(jnp.square(of), axis=-1, keepdims=True) + RMS_EPS)
        o = (of * subln_g[l].astype(jnp.float32) * (1.0 - lam_init)).astype(x.dtype)
        y_attn = o.reshape(bsz, s, D_ATTN) @ w_proj_attn[l]

        mix = (g_a * y_pool + g_b * y_attn) @ w_out[l]
        x = layer_norm(DEEPNORM_ALPHA * x + mix, ln2_g[l], ln2_b[l])

        x = layer_norm(DEEPNORM_ALPHA * x + 0.5 * swiglu(x, ffn2_w_gate[l], ffn2_w_up[l], ffn2_w_down[l]),
                       ln3_g[l], ln3_b[l])
    return x
```

```python
import contextlib
import math
import numpy as np
import ml_dtypes
import concourse.bass as bass
import concourse.mybir as mybir
from concourse.bass_utils import run_bass_kernel_spmd

F32 = mybir.dt.float32
BF16 = mybir.dt.bfloat16
AF = mybir.ActivationFunctionType
ALU = mybir.AluOpType
AX = mybir.AxisListType

D = 1024
S = 8192
DFF = 2816
NFT = DFF // 128
DIN = 5632
ALPHA = 2.0 ** 0.25
LN_EPS = 1e-5
RMS_EPS = 1e-5
LAM_INIT = 0.8 - 0.6 * math.exp(-0.3 * 0)
NEG = -30000.0
SEM_LIMIT = 20000


class Buf:
    __slots__ = ("name", "w", "r", "dsem", "dcnt")

    def __init__(self, name):
        self.name = name
        self.w = {}
        self.r = {}
        self.dsem = None
        self.dcnt = 0


def _merge(dst, src):
    for k, v in src.items():
        if dst.get(k, 0) < v:
            dst[k] = v


class Stream:
    def __init__(self, name):
        self.name = name
        self.ops = []
        self.waited = {}
        self.sem = None
        self.cnt = 0
        self.nsem = 0
        self.own = set()


class Sched:
    def __init__(self, nc, stack):
        self.nc = nc
        self.stack = stack
        self.streams = {n: Stream(n) for n in ("sync", "act", "dve", "pool", "pe")}
        self.alltok = {}

    def new_sem(self, name):
        return self.stack.enter_context(self.nc.semaphore(name))

    def _waits(self, st, deps):
        for sem, val in deps.items():
            if st.waited.get(sem, 0) < val:
                st.waited[sem] = val
                st.ops.append(("wait", sem, val))

    def _deps(self, reads, writes):
        deps = {}
        for b in reads:
            _merge(deps, b.w)
        for b in writes:
            _merge(deps, b.w)
            _merge(deps, b.r)
        return deps

    def _commit(self, tok, reads, writes):
        for b in reads:
            _merge(b.r, tok)
        for b in writes:
            _merge(b.w, tok)
            b.r = {}
        _merge(self.alltok, tok)

    def op(self, sname, fn, reads=(), writes=()):
        st = self.streams[sname]
        deps = self._deps(reads, writes)
        if sname == "pe":
            deps = {k: v for k, v in deps.items() if k not in st.own}
        self._waits(st, deps)
        if st.sem is None or st.cnt >= SEM_LIMIT:
            st.sem = self.new_sem(f"s_{sname}_{st.nsem}")
            st.own.add(st.sem)
            st.nsem += 1
            st.cnt = 0
        st.cnt += 1
        tok = {st.sem: st.cnt}
        st.ops.append(("op", fn, st.sem, 1))
        self._commit(tok, reads, writes)

    def dma(self, fn, sb, reads=(), writes=(), sname="sync"):
        st = self.streams[sname]
        self._waits(st, self._deps(reads, writes))
        if sb.dsem is None or sb.dcnt >= SEM_LIMIT:
            sb.dsem = self.new_sem(f"d_{sb.name}_{sb.dcnt}")
            sb.dcnt = 0
        sb.dcnt += 16
        tok = {sb.dsem: sb.dcnt}
        st.ops.append(("op", fn, sb.dsem, 16))
        self._commit(tok, reads, writes)

    def barrier(self):
        for st in self.streams.values():
            self._waits(st, self.alltok)

    def replay(self, block):
        def mk(st):
            def run(eng):
                for o in st.ops:
                    if o[0] == "wait":
                        eng.wait_ge(o[1], o[2])
                    else:
                        o[1](eng).then_inc(o[2], o[3])
            return run
        block.sync(mk(self.streams["sync"]))
        block.scalar(mk(self.streams["act"]))
        block.vector(mk(self.streams["dve"]))
        block.gpsimd(mk(self.streams["pool"]))
        block.tensor(mk(self.streams["pe"]))


class Prog:
    def __init__(self, cfg):
        self.cfg = cfg
        self.nc = bass.Bass("TRN2", target_bir_lowering=False)
        self.stack = contextlib.ExitStack()
        self.sc = Sched(self.nc, self.stack)
        self.nbuf = 0

    def dram_in(self, name, shape, dt=F32):
        return self.nc.dram_tensor(name, list(shape), dt, kind="ExternalInput").ap()

    def dram_out(self, name, shape, dt=F32):
        return self.nc.dram_tensor(name, list(shape), dt, kind="ExternalOutput").ap()

    def dram_tmp(self, name, shape, dt):
        kind = "ExternalOutput" if name in self.cfg.get("debug_out", ()) else "Internal"
        return self.nc.dram_tensor(name, list(shape), dt, kind=kind).ap()

    def sb(self, st, name, shape, dt):
        return st.enter_context(self.nc.sbuf_tensor("sb_" + name, list(shape), dt))

    def buf(self, name):
        self.nbuf += 1
        return Buf(f"{name}{self.nbuf}")

    def load_weight(self, st, w_ap, K, F, wb, wbuf, stages, f_chunk=2048, col0=0):
        sc = self.sc
        wv = w_ap.rearrange("(kt p) f -> p kt f", p=128)
        i = self._wl_i if hasattr(self, "_wl_i") else 0
        for kt in range(K // 128):
            for f0 in range(0, F, f_chunk):
                f1 = min(F, f0 + f_chunk)
                stg, sbuf_ = stages[i % len(stages)]
                sc.dma(lambda e, o=stg[:, 0:f1 - f0], s=wv[:, kt, col0 + f0:col0 + f1]: e.dma_start(out=o, in_=s),
                       sbuf_, writes=[sbuf_])
                if i % 2 == 0:
                    sc.op("dve", lambda e, o=wb[:, kt, f0:f1], s=stg[:, 0:f1 - f0]: e.tensor_copy(out=o, in_=s),
                          reads=[sbuf_], writes=[wbuf])
                else:
                    sc.op("act", lambda e, o=wb[:, kt, f0:f1], s=stg[:, 0:f1 - f0]: e.copy(out=o, in_=s),
                          reads=[sbuf_], writes=[wbuf])
                i += 1
        self._wl_i = i

    def load_weight_cols(self, w_ap, K, f0, f1, wb, wbuf, stages):
        sc = self.sc
        wv = w_ap.rearrange("(kt p) f -> p kt f", p=128)
        i = self._wl_i if hasattr(self, "_wl_i") else 0
        for kt in range(K // 128):
            stg, sbuf_ = stages[i % len(stages)]
            sc.dma(lambda e, o=stg[:, 0:f1 - f0], s=wv[:, kt, f0:f1]: e.dma_start(out=o, in_=s), sbuf_, writes=[sbuf_])
            if i % 2 == 0:
                sc.op("dve", lambda e, o=wb[:, kt, f0:f1], s=stg[:, 0:f1 - f0]: e.tensor_copy(out=o, in_=s),
                      reads=[sbuf_], writes=[wbuf])
            else:
                sc.op("act", lambda e, o=wb[:, kt, f0:f1], s=stg[:, 0:f1 - f0]: e.copy(out=o, in_=s),
                      reads=[sbuf_], writes=[wbuf])
            i += 1
        self._wl_i = i


def build(cfg):
    P = Prog(cfg)
    nc, sc = P.nc, P.sc
    NCH = cfg.get("n_chunks", 8)
    NOWN = NCH * 512
    NTOK = 2 * NOWN
    NH = cfg.get("n_heads", 8)
    phases = cfg.get("phases", ("A1", "A2", "B", "C1", "C2"))

    xT = P.dram_in("xT", [D, NTOK])
    cols = P.dram_in("cols", [128, 80])
    ones_d = P.dram_in("ones", [128, 128], BF16)
    ident_d = P.dram_in("ident", [128, 128], BF16)
    tri_d = P.dram_in("tri", [128, 128], BF16)
    omask_d = P.dram_in("omask", [128, 512], BF16)
    kconst = P.dram_in("kconst", [8, 5, NTOK], BF16)
    qconst = P.dram_in("qconst", [8, 5, NOWN], BF16)
    btab_d = P.dram_in("btab", [8, 128, (NTOK // 128) * NCH])
    icnt_d = P.dram_in("icnt", [128, 4, 512])
    lamv_d = P.dram_in("lamv", [128, 4, 64])
    ones1_d = P.dram_in("ones1", [128, 128], BF16)
    ones128_d = P.dram_in("ones128", [128, 128], BF16)
    w1g = P.dram_in("w1g", [D, DFF]); w1u = P.dram_in("w1u", [D, DFF]); w1d = P.dram_in("w1d", [DFF, D])
    w2g = P.dram_in("w2g", [D, DFF]); w2u = P.dram_in("w2u", [D, DFF]); w2d = P.dram_in("w2d", [DFF, D])
    win = P.dram_in("win", [D, DIN])
    poolw_d = P.dram_in("poolw", [128, 4, 128])
    wpp_d = P.dram_in("wpp", [512, D]); wpa_d = P.dram_in("wpa", [D, D]); wout_d = P.dram_in("wout", [D, D])
    outT = P.dram_out("outT", [D, NOWN])
    x1T = P.dram_tmp("x1T", [D, NOWN], F32)
    x1bT = P.dram_tmp("x1bT", [D, NTOK], BF16)
    KTs = P.dram_tmp("KTs", [1024, NTOK], BF16)
    Vs = P.dram_tmp("Vs", [NTOK, 1024], BF16)
    UTs = P.dram_tmp("UTs", [512, NTOK], F32)
    QTs = P.dram_tmp("QTs", [1024, NOWN], BF16)
    OTs = P.dram_tmp("OTs", [1024, NOWN], BF16)
    x2T = P.dram_tmp("x2T", [D, NOWN], F32)
    B_in = P.buf("inputs")
    B_x1T = P.buf("x1T"); B_x1bT = P.buf("x1bT"); B_KTs = P.buf("KTs"); B_Vs = P.buf("Vs"); B_UTs = P.buf("UTs")
    B_QTs = P.buf("QTs"); B_OTs = P.buf("OTs"); B_x2T = P.buf("x2T"); B_out = P.buf("outT")

    with P.stack:
        st0 = P.stack
        cols_sb = P.sb(st0, "cols_sb", [128, 80], F32); B_cols = P.buf("cols")
        ones_sb = P.sb(st0, "ones_sb", [128, 128], BF16); B_ones = P.buf("ones")
        ident_sb = P.sb(st0, "ident_sb", [128, 128], BF16)
        tri_sb = P.sb(st0, "tri_sb", [128, 128], BF16)
        omask_sb = P.sb(st0, "omask_sb", [128, 512], BF16)
        B_cst = P.buf("cst")
        lamv = P.sb(st0, "lamv", [128, 4, 64], F32)
        ones1_sb = P.sb(st0, "ones1_sb", [128, 128], BF16)
        ones128_sb = P.sb(st0, "ones128_sb", [128, 128], BF16)
        ltmp = P.sb(st0, "ltmp", [128, 2, 64], F32)
        lsc = P.sb(st0, "lsc", [128, 8], F32)
        bs_sb = P.sb(st0, "bs_sb", [128, 4], F32)
        B_lam = P.buf("lam")
        sc.dma(lambda e: e.dma_start(out=cols_sb[:], in_=cols[:]), B_cols, writes=[B_cols])
        sc.dma(lambda e: e.dma_start(out=ones_sb[:], in_=ones_d[:]), B_ones, writes=[B_ones])
        sc.dma(lambda e: e.dma_start(out=ident_sb[:], in_=ident_d[:]), B_cst, writes=[B_cst])
        sc.dma(lambda e: e.dma_start(out=tri_sb[:], in_=tri_d[:]), B_cst, writes=[B_cst])
        sc.dma(lambda e: e.dma_start(out=omask_sb[:], in_=omask_d[:]), B_cst, writes=[B_cst])
        sc.dma(lambda e: e.dma_start(out=lamv[:], in_=lamv_d[:]), B_lam, writes=[B_lam])
        sc.dma(lambda e: e.dma_start(out=ones1_sb[:], in_=ones1_d[:]), B_cst, writes=[B_cst])
        sc.dma(lambda e: e.dma_start(out=ones128_sb[:], in_=ones128_d[:]), B_cst, writes=[B_cst])
        sc.op("dve", lambda e: e.tensor_tensor(out=ltmp[:, 0, :], in0=lamv[:, 0, :], in1=lamv[:, 1, :], op=ALU.mult),
              reads=[B_lam], writes=[B_lam])
        sc.op("dve", lambda e: e.tensor_tensor(out=ltmp[:, 1, :], in0=lamv[:, 2, :], in1=lamv[:, 3, :], op=ALU.mult),
              reads=[B_lam], writes=[B_lam])
        sc.op("dve", lambda e: e.reduce_sum(out=lsc[:, 0:1], in_=ltmp[:, 0, :], axis=AX.X), reads=[B_lam], writes=[B_lam])
        sc.op("dve", lambda e: e.reduce_sum(out=lsc[:, 1:2], in_=ltmp[:, 1, :], axis=AX.X), reads=[B_lam], writes=[B_lam])
        sc.op("act", lambda e: e.activation(out=lsc[:, 2:4], in_=lsc[:, 0:2], func=AF.Exp), reads=[B_lam], writes=[B_lam])
        sc.op("dve", lambda e: e.tensor_tensor(out=lsc[:, 4:5], in0=lsc[:, 2:3], in1=lsc[:, 3:4], op=ALU.subtract),
              reads=[B_lam], writes=[B_lam])
        sc.op("dve", lambda e: e.tensor_scalar(out=lsc[:, 5:6], in0=lsc[:, 4:5], scalar1=LAM_INIT, scalar2=-1.0,
                                              op0=ALU.add, op1=ALU.mult), reads=[B_lam], writes=[B_lam])
        sc.op("dve", lambda e: e.tensor_scalar(out=lsc[:, 6:7], in0=cols_sb[:, 74:75], scalar1=1.0 - LAM_INIT, scalar2=None,
                                              op0=ALU.mult), reads=[B_lam, B_cols], writes=[B_lam])
        sc.op("dve", lambda e: e.tensor_tensor(out=bs_sb[:], in0=cols_sb[:, 64:68], in1=cols_sb[:, 68:72], op=ALU.mult),
              reads=[B_cols], writes=[B_cols])
        neglam = lsc[:, 5:6]
        gcolv = lsc[:, 6:7]

        def residual_ln(pfx, st, psum, PB, xfs, B_xs, T, yprod, c_res, gcol, bcol, obs=None, B_obs=None):
            zb = [P.sb(st, f"{pfx}zb{i}", [128, T], BF16) for i in range(2)]
            B_zb = [P.buf("zb") for _ in range(2)]
            zq = [P.sb(st, f"{pfx}zq{i}", [128, T], BF16) for i in range(2)]
            B_zq = [P.buf("zq") for _ in range(2)]
            mean = P.sb(st, pfx + "mean", [128, T], F32); B_mean = P.buf("mean")
            sd = P.sb(st, pfx + "sd", [128, T], F32); B_sd = P.buf("sd")
            msq = P.sb(st, pfx + "msq", [128, T], F32); B_msq = P.buf("msq")
            eps_eff = LN_EPS / (ALPHA * ALPHA)

            def run(xfs, B_xs, obs, B_obs, hook=None):
                def stats(fo):
                    q = fo % 2

                    def mm_st(e, q=q, fo=fo):
                        e.matmul(psum[:, 6, 0:T], ones_sb[:], zb[q][:], start=(fo == 0), stop=(fo == 7))
                        return e.matmul(psum[:, 7, 0:T], ones_sb[:], zq[q][:], start=(fo == 0), stop=(fo == 7))
                    sc.op("pe", mm_st, reads=[B_ones, B_zb[q], B_zq[q]], writes=[PB[6], PB[7]])
                for fo in range(8):
                    py, by = yprod(fo)
                    if fo >= 1:
                        stats(fo - 1)
                    zs = xfs[:, fo, :]
                    sc.op("dve", lambda e, zs=zs, py=py: e.scalar_tensor_tensor(out=zs, in0=py, scalar=c_res, in1=zs,
                                                                                 op0=ALU.mult, op1=ALU.add),
                          reads=[by, B_xs[fo]], writes=[B_xs[fo]])
                    q = fo % 2
                    sc.op("dve", lambda e, q=q, zs=zs: e.tensor_copy(out=zb[q][:], in_=zs), reads=[B_xs[fo]], writes=[B_zb[q]])
                    sc.op("dve", lambda e, q=q, zs=zs: e.tensor_tensor(out=zq[q][:], in0=zs, in1=zs, op=ALU.mult),
                          reads=[B_xs[fo]], writes=[B_zq[q]])
                    if fo == 2 and hook is not None:
                        hook()
                stats(7)

                def st_stats():
                    sc.op("dve", lambda e: e.tensor_copy(out=mean[:], in_=psum[:, 6, 0:T]), reads=[PB[6]], writes=[B_mean])
                    sc.op("dve", lambda e: e.tensor_tensor(out=msq[:], in0=mean[:], in1=mean[:], op=ALU.mult),
                          reads=[B_mean], writes=[B_msq])
                    sc.op("dve", lambda e: e.tensor_tensor(out=msq[:], in0=psum[:, 7, 0:T], in1=msq[:], op=ALU.subtract),
                          reads=[PB[7], B_msq], writes=[B_msq])
                    sc.op("act", lambda e: e.activation(out=sd[:], in_=msq[:], func=AF.Sqrt, bias=eps_eff, scale=1.0),
                          reads=[B_msq], writes=[B_sd])

                def st_recip():
                    sc.op("dve", lambda e: e.reciprocal(out=sd[:], in_=sd[:]), reads=[B_sd], writes=[B_sd])

                def st_norm(which, half):
                    def f():
                        ks = slice(4 * half, 4 * half + 4)
                        xa = xfs[:, ks, :]
                        src = mean if which == 0 else sd
                        bsrc = B_mean if which == 0 else B_sd
                        op = ALU.subtract if which == 0 else ALU.mult
                        sc.op("dve", lambda e: e.tensor_tensor(out=xa, in0=xa, in1=src[:].unsqueeze(1).broadcast_to([128, 4, T]),
                                                               op=op), reads=list(B_xs[ks]) + [bsrc], writes=list(B_xs[ks]))
                    return f

                def st_aff(fo):
                    def f():
                        zs = xfs[:, fo, :]
                        sc.op("act", lambda e: e.activation(out=zs, in_=zs, func=AF.Identity,
                                                            bias=cols_sb[:, bcol + fo:bcol + fo + 1],
                                                            scale=cols_sb[:, gcol + fo:gcol + fo + 1]),
                              reads=[B_xs[fo], B_cols], writes=[B_xs[fo]])
                        if obs is not None:
                            sc.op("act", lambda e: e.copy(out=obs[:, fo, :], in_=zs), reads=[B_xs[fo]], writes=[B_obs[fo]])
                    return f
                steps = [st_stats, st_recip, st_norm(0, 0), st_norm(0, 1), st_norm(1, 0), st_norm(1, 1)]
                steps += [st_aff(fo) for fo in range(8)]
                return steps
            return run

        def ffn_phase(name, src, B_src, wg_d, wu_d, wd_d, gcol, bcol, ntiles, dst_f32, B_dst32, n32, dst_bf16, B_dst16):
            T = 256
            with contextlib.ExitStack() as st:
                psum = st.enter_context(nc.psum_tensor(name + "ps", [128, 8, 512], F32))
                PB = [P.buf(f"psum{i}") for i in range(8)]
                wg = P.sb(st, name + "wg", [128, 8, DFF], BF16); B_wg = P.buf("wg")
                wu = P.sb(st, name + "wu", [128, 8, DFF], BF16); B_wu = P.buf("wu")
                wd = P.sb(st, name + "wd", [128, NFT, D], BF16); B_wd = P.buf("wd")
                stages = [(P.sb(st, f"{name}stg{i}", [128, 2048], F32), P.buf("stg")) for i in range(2)]
                xf = [P.sb(st, f"{name}xf{i}", [128, 8, T], F32) for i in range(2)]
                B_xf = [[P.buf("xf") for _ in range(8)] for _ in range(2)]
                xb = [P.sb(st, f"{name}xb{i}", [128, 8, T], BF16) for i in range(2)]
                B_xb = [P.buf("xb") for _ in range(2)]
                ob = [P.sb(st, f"{name}ob{i}", [128, 8, T], BF16) for i in range(2)] if dst_bf16 is not None else [None, None]
                B_ob = [[P.buf("ob") for _ in range(8)] for _ in range(2)]
                hT = P.sb(st, name + "hT", [128, NFT, T], BF16)
                B_h = [P.buf("h") for _ in range(NFT)]
                sg = [P.sb(st, f"{name}sg{i}", [128, T], F32) for i in range(4)]
                B_sg = [P.buf("sg") for _ in range(4)]
                srcv = src.rearrange("(kt p) t -> p kt t", p=128)

                def load_x(t):
                    s = t % 2
                    sc.dma(lambda e, s=s, t=t: e.dma_start(out=xf[s][:], in_=srcv[:, :, t * T:(t + 1) * T]), B_xb[s],
                           reads=[B_src], writes=B_xf[s])

                def cast_x(t):
                    s = t % 2
                    sc.op("act", lambda e, s=s: e.copy(out=xb[s][:], in_=xf[s][:]), reads=B_xf[s], writes=[B_xb[s]])

                load_x(0)
                B_wgc = [P.buf("wgc") for _ in range(2)]
                B_wuc = [P.buf("wuc") for _ in range(2)]
                P.load_weight_cols(wg_d, D, 0, 2048, wg, B_wgc[0], stages)
                cast_x(0)
                if ntiles > 1:
                    load_x(1)
                P.load_weight_cols(wu_d, D, 0, 2048, wu, B_wuc[0], stages)
                P.load_weight_cols(wg_d, D, 2048, DFF, wg, B_wgc[1], stages)
                P.load_weight_cols(wu_d, D, 2048, DFF, wu, B_wuc[1], stages)
                P.load_weight(st, wd_d, DFF, D, wd, B_wd, stages)

                def yprod(fo):
                    by = PB[4 + fo % 2]
                    py = psum[:, 4 + fo % 2, 0:T]

                    def mm_d(e, fo=fo, py=py):
                        for kt in range(NFT):
                            ins = e.matmul(py, wd[:, kt, fo * 128:(fo + 1) * 128], hT[:, kt, :],
                                           start=(kt == 0), stop=(kt == NFT - 1))
                        return ins
                    sc.op("pe", mm_d, reads=[B_wd] + B_h, writes=[by])
                    return py, by
                ln_run = residual_ln(name, st, psum, PB, None, None, T, yprod, 0.5 / ALPHA, gcol, bcol)

                def finish(t):
                    s = t % 2
                    if dst_f32 is not None and t < n32:
                        dv = dst_f32.rearrange("(kt p) t -> p kt t", p=128)[:, :, t * T:(t + 1) * T]
                        sc.dma(lambda e, dv=dv, s=s: e.dma_start(out=dv, in_=xf[s][:]), B_xb[s], reads=B_xf[s], writes=[B_dst32])
                    if dst_bf16 is not None:
                        dv16 = dst_bf16.rearrange("(kt p) t -> p kt t", p=128)[:, :, t * T:(t + 1) * T]
                        sc.dma(lambda e, dv16=dv16, s=s: e.dma_start(out=dv16, in_=ob[s][:]), B_ob[s][0], reads=B_ob[s],
                               writes=[B_dst16])

                pending = None
                for t in range(ntiles):
                    s = t % 2
                    for ft in range(NFT):
                        if ft >= 2 and pending is not None:
                            if pending[1]:
                                pending[1].pop(0)()
                            else:
                                finish(pending[0])
                                pending = None
                                if t + 1 < ntiles:
                                    load_x(t + 1)
                        i0 = ft % 4
                        bg = bu = PB[i0]
                        pg = psum[:, i0, 0:T]
                        pu = psum[:, i0, T:2 * T]

                        def mm_gu(e, ft=ft, pg=pg, pu=pu, s=s):
                            for kt in range(8):
                                e.matmul(pg, wg[:, kt, ft * 128:(ft + 1) * 128], xb[s][:, kt, :],
                                         start=(kt == 0), stop=(kt == 7))
                            for kt in range(8):
                                ins = e.matmul(pu, wu[:, kt, ft * 128:(ft + 1) * 128], xb[s][:, kt, :],
                                               start=(kt == 0), stop=(kt == 7))
                            return ins
                        fc = 0 if ft < 16 else 1
                        sc.op("pe", mm_gu, reads=[B_wgc[fc], B_wuc[fc], B_xb[s]], writes=[bg])
                        q = ft % 4
                        sc.op("act", lambda e, q=q, pg=pg: e.activation(out=sg[q][:], in_=pg, func=AF.Silu),
                              reads=[bg], writes=[B_sg[q]])
                        sc.op("dve", lambda e, q=q, pu=pu, ft=ft: e.tensor_tensor(out=hT[:, ft, :], in0=sg[q][:], in1=pu,
                                                                                 op=ALU.mult),
                              reads=[B_sg[q], bu], writes=[B_h[ft]])
                    if pending is not None:
                        while pending[1]:
                            pending[1].pop(0)()
                        finish(pending[0])
                        pending = None
                        if t + 1 < ntiles:
                            load_x(t + 1)
                    hook = (lambda t=t: cast_x(t + 1)) if t + 1 < ntiles else None
                    steps = ln_run(xf[s], B_xf[s], ob[s] if (dst_bf16 is not None) else None, B_ob[s], hook)
                    pending = (t, steps)
                while pending[1]:
                    pending[1].pop(0)()
                finish(pending[0])
                sc.barrier()

        def phase_a2():
            T = 512
            with contextlib.ExitStack() as st:
                psum = st.enter_context(nc.psum_tensor("a2ps", [128, 8, 512], F32))
                PB = [P.buf(f"psum{i}") for i in range(8)]
                w = P.sb(st, "a2w", [128, 8, 3584], BF16); B_w = P.buf("a2w")
                stages = [(P.sb(st, f"a2stg{i}", [128, 2048], F32), P.buf("stg")) for i in range(2)]
                xb = [P.sb(st, f"a2xb{i}", [128, 8, T], BF16) for i in range(2)]
                B_xb = [P.buf("xb") for _ in range(2)]
                kst = [P.sb(st, f"a2k{i}", [128, 8, T], BF16) for i in range(2)]
                B_k = [[P.buf("k") for _ in range(8)] for _ in range(2)]
                qst = [P.sb(st, f"a2q{i}", [128, 8, T], BF16) for i in range(2)]
                B_q = [[P.buf("q") for _ in range(8)] for _ in range(2)]
                ust = [P.sb(st, f"a2u{i}", [128, 4, T], F32) for i in range(2)]
                B_u = [[P.buf("u") for _ in range(4)] for _ in range(2)]
                vst = [P.sb(st, f"a2v{i}", [128, 4, 1024], BF16) for i in range(2)]
                B_v = [[P.buf("v") for _ in range(8)] for _ in range(2)]
                srcv = x1bT.rearrange("(kt p) t -> p kt t", p=128)
                ntiles = NTOK // T

                def load_x(t):
                    s = t % 2
                    sc.dma(lambda e, s=s, t=t: e.dma_start(out=xb[s][:], in_=srcv[:, :, t * T:(t + 1) * T]), B_xb[s],
                           reads=[B_x1bT], writes=[B_xb[s]])
                load_x(0)
                B_wc = [P.buf("a2wc") for _ in range(2)]
                P.load_weight_cols(win, D, 0, 2048, w, B_wc[0], stages)
                P.load_weight_cols(win, D, 2048, 3584, w, B_wc[1], stages)
                cnt = [0]

                def proj(s, col, dst_ap, B_dst, scale=None, c0=0):
                    k = cnt[0] % 8
                    cnt[0] += 1
                    pp = psum[:, k, c0:T]
                    dst_ap = dst_ap[:, c0:T]

                    def mm(e, col=col, pp=pp, s=s):
                        for kt in range(8):
                            ins = e.matmul(pp, w[:, kt, col:col + 128], xb[s][:, kt, c0:T], start=(kt == 0), stop=(kt == 7))
                        return ins
                    sc.op("pe", mm, reads=[B_wc[0 if col < 2048 else 1], B_xb[s]], writes=[PB[k]])
                    if k % 2 == 0:
                        if scale is None:
                            fn = lambda e: e.copy(out=dst_ap, in_=pp)
                        else:
                            fn = lambda e: e.mul(out=dst_ap, in_=pp, mul=scale)
                        sc.op("act", fn, reads=[PB[k]], writes=[B_dst])
                    else:
                        if scale is None:
                            fn = lambda e: e.tensor_copy(out=dst_ap, in_=pp)
                        else:
                            fn = lambda e: e.tensor_scalar(out=dst_ap, in0=pp, scalar1=scale, scalar2=None, op0=ALU.mult)
                        sc.op("dve", fn, reads=[PB[k]], writes=[B_dst])

                for t in range(ntiles):
                    s = t % 2
                    if t + 1 < ntiles:
                        load_x(t + 1)
                    tk = slice(t * T, (t + 1) * T)
                    for ft in range(4):
                        if t < NOWN // T:
                            proj(s, ft * 128, ust[s][:, ft, :], B_u[s][ft])
                        else:
                            proj(s, ft * 128, ust[s][:, ft, :], B_u[s][ft], c0=T - 16)
                    sc.dma(lambda e, s=s, tk=tk: e.dma_start(out=UTs.rearrange("(g p) t -> p g t", p=128)[:, :, tk], in_=ust[s][:]),
                           B_u[s][0], reads=B_u[s], writes=[B_UTs])
                    if t < NOWN // T:
                        for ft in range(8):
                            proj(s, 512 + ft * 128, qst[s][:, ft, :], B_q[s][ft], scale=0.125)
                        sc.dma(lambda e, s=s, tk=tk: e.dma_start(out=QTs.rearrange("(g p) t -> p g t", p=128)[:, :, tk], in_=qst[s][:]),
                               B_q[s][0], reads=B_q[s], writes=[B_QTs])
                    for ft in range(8):
                        proj(s, 1536 + ft * 128, kst[s][:, ft, :], B_k[s][ft])
                    sc.dma(lambda e, s=s, tk=tk: e.dma_start(out=KTs.rearrange("(g p) t -> p g t", p=128)[:, :, tk], in_=kst[s][:]),
                           B_k[s][0], reads=B_k[s], writes=[B_KTs])
                    for ts in range(4):
                        for fc in range(2):
                            k = cnt[0] % 8
                            cnt[0] += 1
                            pp = psum[:, k, :]

                            def mmv(e, ts=ts, fc=fc, pp=pp, s=s):
                                for kt in range(8):
                                    ins = e.matmul(pp, xb[s][:, kt, ts * 128:(ts + 1) * 128],
                                                   w[:, kt, 2560 + fc * 512:2560 + (fc + 1) * 512],
                                                   start=(kt == 0), stop=(kt == 7))
                                return ins
                            sc.op("pe", mmv, reads=[B_wc[1], B_xb[s]], writes=[PB[k]])
                            dst_ap = vst[s][:, ts, fc * 512:(fc + 1) * 512]
                            if k % 2 == 0:
                                sc.op("act", lambda e, dst_ap=dst_ap, pp=pp: e.copy(out=dst_ap, in_=pp), reads=[PB[k]],
                                      writes=[B_v[s][ts * 2 + fc]])
                            else:
                                sc.op("dve", lambda e, dst_ap=dst_ap, pp=pp: e.tensor_copy(out=dst_ap, in_=pp), reads=[PB[k]],
                                      writes=[B_v[s][ts * 2 + fc]])
                    sc.dma(lambda e, s=s, t=t: e.dma_start(out=Vs[t * T:(t + 1) * T, :].rearrange("(ts p) f -> p ts f", p=128),
                                                           in_=vst[s][:]),
                           B_v[s][0], reads=B_v[s], writes=[B_Vs])
                sc.barrier()

        def phase_b():
            NKB = NTOK // 128
            PACK_FROM = cfg.get("pack_from", 2)
            with contextlib.ExitStack() as st:
                psum = st.enter_context(nc.psum_tensor("bps", [128, 8, 512], F32))
                PS2 = [P.buf("S2a"), P.buf("S2b")]
                B_O = P.buf("O"); B_L0 = P.buf("L0"); B_L1 = P.buf("L1")
                KT = [[P.sb(st, f"bK{sl}{m}", [128, NTOK], BF16) for m in range(2)] for sl in range(2)]
                QT = [[P.sb(st, f"bQ{sl}{m}", [128, NOWN], BF16) for m in range(2)] for sl in range(2)]
                VA = [P.sb(st, f"bV{sl}", [128, NKB, 128], BF16) for sl in range(2)]
                btab = [P.sb(st, f"bbt{sl}", [128, NKB * NCH], F32) for sl in range(2)]
                B_hd = [P.buf("hd") for _ in range(2)]
                NPT = 6
                PT = [P.sb(st, f"bP{i}", [128, 2, 512], BF16) for i in range(NPT)]
                B_PT = [P.buf("PT") for _ in range(NPT)]
                Osb = P.sb(st, "bOsb", [128, 2, 512], F32); B_Osb = P.buf("Osb")
                L1sb = P.sb(st, "bL1sb", [128, 512], F32); B_L1sb = P.buf("L1sb")
                rL = P.sb(st, "brL", [128, 2, 512], F32); B_rL = P.buf("rL")
                nb = P.sb(st, "bnb", [128, 2, 512], F32); B_nb = P.buf("nb")
                d_t = P.sb(st, "bdt", [128, 512], F32); B_d = P.buf("d")
                dsq = P.sb(st, "bdsq", [128, 512], BF16); B_dsq = P.buf("dsq")
                rs = P.sb(st, "brs", [128, 512], F32); B_rs = P.buf("rs")
                OTst = [P.sb(st, f"bOT{i}", [128, 512], BF16) for i in range(2)]; B_OT = [P.buf("OTst") for _ in range(2)]
                acc = [P.sb(st, f"bacc{i}", [128, 512], F32) for i in range(2)]
                B_acc = [P.buf("acc") for _ in range(2)]
                hi = P.sb(st, "bhi", [128, 512], BF16); B_hi = P.buf("hi")
                lo = P.sb(st, "blo", [128, 512], BF16); B_lo = P.buf("lo")

                def load_head(h):
                    sl = h % 2
                    if h >= PACK_FROM:
                        for m in range(2):
                            r0 = m * 512 + h * 64
                            sc.dma(lambda e, sl=sl, m=m, r0=r0: e.dma_start(out=KT[sl][0][m * 64:(m + 1) * 64, :], in_=KTs[r0:r0 + 64, :]),
                                   B_hd[sl], reads=[B_KTs], writes=[B_hd[sl]])
                            sc.dma(lambda e, sl=sl, m=m, r0=r0: e.dma_start(out=QT[sl][0][m * 64:(m + 1) * 64, :], in_=QTs[r0:r0 + 64, :]),
                                   B_hd[sl], reads=[B_QTs], writes=[B_hd[sl]])
                        sc.dma(lambda e, sl=sl, h=h: e.dma_start(out=btab[sl][:], in_=btab_d[h]),
                               B_hd[sl], reads=[B_in], writes=[B_hd[sl]])
                    else:
                        for m in range(2):
                            r0 = m * 512 + h * 64
                            sc.dma(lambda e, sl=sl, m=m, r0=r0: e.dma_start(out=KT[sl][m][0:64, :], in_=KTs[r0:r0 + 64, :]),
                                   B_hd[sl], reads=[B_KTs], writes=[B_hd[sl]])
                            sc.dma(lambda e, sl=sl, m=m, h=h: e.dma_start(out=KT[sl][m][64:69, :], in_=kconst[h]),
                                   B_hd[sl], reads=[B_in], writes=[B_hd[sl]])
                            sc.dma(lambda e, sl=sl, m=m, r0=r0: e.dma_start(out=QT[sl][m][0:64, :], in_=QTs[r0:r0 + 64, :]),
                                   B_hd[sl], reads=[B_QTs], writes=[B_hd[sl]])
                            sc.dma(lambda e, sl=sl, m=m, h=h: e.dma_start(out=QT[sl][m][64:69, :], in_=qconst[h]),
                                   B_hd[sl], reads=[B_in], writes=[B_hd[sl]])
                    vv = Vs[:, h * 128:(h + 1) * 128].rearrange("(j p) d -> p j d", p=128)
                    for j0 in range(0, NKB, 4):
                        sc.dma(lambda e, sl=sl, j0=j0, vv=vv: e.dma_start(out=VA[sl][:, j0:j0 + 4, :], in_=vv[:, j0:j0 + 4, :]),
                               B_hd[sl], reads=[B_Vs], writes=[B_hd[sl]])

                jobs = []
                for h in range(NH):
                    for i in range(NCH):
                        ab_ = (h * NCH + i) % 2
                        blocks = [("other", i2, jj) for i2 in range(i + 1) for jj in range(4)]
                        blocks += [("own", i2, jj) for i2 in range(i) for jj in range(4)]
                        blocks += [("diag", i, jj) for jj in range(4)]
                        for bi, (kind, i2, jj) in enumerate(blocks):
                            jobs.append(dict(h=h, sl=h % 2, i=i, bi=bi, nblk=len(blocks), kind=kind, ab=ab_,
                                             kcol=(NOWN if kind == "other" else 0) + i2 * 512 + jj * 128,
                                             q0=(jj * 128 if kind == "diag" else 0),
                                             omk=(kind == "other" and i2 == i)))

                def emit_qk(j, n):
                    sidx = n % 2
                    sl, kcol, q0, kind, omk, i, h = j["sl"], j["kcol"], j["q0"], j["kind"], j["omk"], j["i"], j["h"]
                    packed = h >= PACK_FROM

                    def mm_qk(e):
                        for m in range(2):
                            if packed:
                                ins = e.matmul(psum[:, 2 * sidx + m, q0:512], KT[sl][0][m * 64:(m + 1) * 64, kcol:kcol + 128],
                                               QT[sl][0][m * 64:(m + 1) * 64, i * 512 + q0:(i + 1) * 512],
                                               start=True, stop=not (kind == "diag"))
                            else:
                                ins = e.matmul(psum[:, 2 * sidx + m, q0:512], KT[sl][m][0:69, kcol:kcol + 128],
                                               QT[sl][m][0:69, i * 512 + q0:(i + 1) * 512],
                                               start=True, stop=not (kind == "diag" or omk))
                        for m in range(2):
                            if kind == "diag":
                                ins = e.matmul(psum[:, 2 * sidx + m, q0:q0 + 128], ident_sb[:], tri_sb[:],
                                               start=False, stop=True)
                            elif omk and not packed:
                                ins = e.matmul(psum[:, 2 * sidx + m, 0:512], ident_sb[:], omask_sb[:],
                                               start=False, stop=True)
                        return ins
                    sc.op("pe", mm_qk, reads=[B_hd[sl], B_cst], writes=[PS2[sidx]])

                def emit_exp(j, n):
                    sidx, pidx, q0 = n % 2, n % NPT, j["q0"]
                    if j["h"] >= PACK_FROM:
                        col = (j["kcol"] // 128) * NCH + j["i"]
                        sl = j["sl"]
                        sc.op("act", lambda e: e.activation(out=PT[pidx][:, :, q0:512], in_=psum[:, 2 * sidx:2 * sidx + 2, q0:512],
                                                            func=AF.Exp, bias=btab[sl][:, col:col + 1], scale=1.0),
                              reads=[PS2[sidx], B_hd[sl]], writes=[B_PT[pidx]])
                    else:
                        sc.op("act", lambda e: e.activation(out=PT[pidx][:, :, q0:512], in_=psum[:, 2 * sidx:2 * sidx + 2, q0:512],
                                                            func=AF.Exp), reads=[PS2[sidx]], writes=[B_PT[pidx]])

                def emit_pv(j, n):
                    pidx = n % NPT
                    sl, kcol, q0, bi, nblk = j["sl"], j["kcol"], j["q0"], j["bi"], j["nblk"]

                    def mm_pv(e):
                        for m in range(2):
                            e.matmul(psum[:, 4 + m, q0:512], VA[sl][:, kcol // 128, :], PT[pidx][:, m, q0:512],
                                     start=(bi == 0), stop=(bi == nblk - 1), skip_group_check=True)
                        return e.matmul(psum[:, 7, q0:512], ones1_sb[:], PT[pidx][:, 1, q0:512],
                                        start=(bi == 0), stop=(bi == nblk - 1), skip_group_check=True)
                    sc.op("pe", mm_pv, reads=[B_PT[pidx], B_hd[sl], B_cst], writes=[B_O, B_L1])
                    ab = j["ab"]
                    if bi == 0:
                        sc.op("dve", lambda e: e.tensor_copy(out=acc[ab][:], in_=PT[pidx][:, 0, :]),
                              reads=[B_PT[pidx]], writes=[B_acc[ab]])
                    else:
                        sc.op("dve", lambda e: e.tensor_tensor(out=acc[ab][:, q0:512], in0=acc[ab][:, q0:512],
                                                               in1=PT[pidx][:, 0, q0:512], op=ALU.add),
                              reads=[B_PT[pidx], B_acc[ab]], writes=[B_acc[ab]])

                ep = [0]

                def epilogue_stages(j):
                    h, i, ab = j["h"], j["i"], j["ab"]
                    o = ep[0] % 2
                    ep[0] += 1

                    def s0():
                        sc.op("dve", lambda e: e.tensor_copy(out=Osb[:], in_=psum[:, 4:6, :]), reads=[B_O], writes=[B_Osb])
                        sc.op("dve", lambda e: e.tensor_copy(out=L1sb[:], in_=psum[:, 7, :]), reads=[B_L1], writes=[B_L1sb])
                        sc.op("dve", lambda e: e.tensor_copy(out=hi[:], in_=acc[ab][:]), reads=[B_acc[ab]], writes=[B_hi])
                        sc.op("dve", lambda e: e.tensor_tensor(out=lo[:], in0=acc[ab][:], in1=hi[:], op=ALU.subtract),
                              reads=[B_acc[ab], B_hi], writes=[B_lo])

                    def s1a():
                        def mm_L(e):
                            e.matmul(psum[:, 6, :], ones1_sb[:], hi[:], start=True, stop=False)
                            return e.matmul(psum[:, 6, :], ones1_sb[:], lo[:], start=False, stop=True)
                        sc.op("pe", mm_L, reads=[B_hi, B_lo, B_cst], writes=[B_L0])
                        sc.op("dve", lambda e: e.reciprocal(out=rL[:, 1, :], in_=L1sb[:]), reads=[B_L1sb, B_rL], writes=[B_rL])

                    def s1b():
                        sc.op("dve", lambda e: e.reciprocal(out=rL[:, 0, :], in_=psum[:, 6, :]), reads=[B_L0, B_rL], writes=[B_rL])

                    def s1():
                        sc.op("dve", lambda e: e.tensor_tensor(out=nb[:], in0=Osb[:], in1=rL[:], op=ALU.mult),
                              reads=[B_Osb, B_rL], writes=[B_nb])
                        sc.op("dve", lambda e: e.scalar_tensor_tensor(out=d_t[:], in0=nb[:, 1, :], scalar=neglam, in1=nb[:, 0, :],
                                                                      op0=ALU.mult, op1=ALU.add),
                              reads=[B_nb, B_lam], writes=[B_d])
                        sc.op("dve", lambda e: e.tensor_tensor(out=dsq[:], in0=d_t[:], in1=d_t[:], op=ALU.mult),
                              reads=[B_d], writes=[B_dsq])

                    def s2():
                        sc.op("pe", lambda e: e.matmul(psum[:, 6, :], ones128_sb[:], dsq[:], start=True, stop=True),
                              reads=[B_dsq, B_cst], writes=[B_L0])
                        sc.op("act", lambda e: e.activation(out=rs[:], in_=psum[:, 6, :], func=AF.Ln, bias=RMS_EPS, scale=1.0),
                              reads=[B_L0], writes=[B_rs])
                        sc.op("act", lambda e: e.activation(out=rs[:], in_=rs[:], func=AF.Exp, scale=-0.5),
                              reads=[B_rs], writes=[B_rs])

                    def s3():
                        sc.op("dve", lambda e: e.scalar_tensor_tensor(out=OTst[o][:], in0=d_t[:], scalar=gcolv, in1=rs[:],
                                                                      op0=ALU.mult, op1=ALU.mult),
                              reads=[B_d, B_rs, B_lam], writes=[B_OT[o]])
                        sc.dma(lambda e: e.dma_start(out=OTs[h * 128:(h + 1) * 128, i * 512:(i + 1) * 512], in_=OTst[o][:]),
                               B_OT[o], reads=[B_OT[o]], writes=[B_OTs])
                    return [s0, s1a, s1b, s1, s2, s3]

                load_head(0)
                emit_qk(jobs[0], 0)
                emit_qk(jobs[1], 1)
                pend = []
                for n, j in enumerate(jobs):
                    if j["bi"] == 0 and j["i"] == 0 and j["h"] + 1 < NH:
                        load_head(j["h"] + 1)
                    emit_exp(j, n)
                    if n + 2 < len(jobs):
                        emit_qk(jobs[n + 2], n + 2)
                    emit_pv(j, n)
                    if pend and j["bi"] in (0, 2, 4, 5, 6):
                        pend.pop(0)()
                    if j["bi"] == j["nblk"] - 1:
                        while pend:
                            pend.pop(0)()
                        stages = epilogue_stages(j)
                        stages[0]()
                        pend = stages[1:]
                while pend:
                    pend.pop(0)()
                sc.barrier()

        def phase_c1():
            T = 512
            with contextlib.ExitStack() as st:
                psum = st.enter_context(nc.psum_tensor("c1ps", [128, 8, 512], F32))
                PB = [P.buf(f"psum{i}") for i in range(8)]
                wgate = P.sb(st, "c1wg", [128, 8, 2048], BF16); B_wgt = P.buf("wgate")
                wpl = P.sb(st, "c1wpl", [128, 4, 128], BF16); B_wpl = P.buf("wpl")
                wpp = P.sb(st, "c1wpp", [128, 4, D], BF16); B_wpp = P.buf("wpp")
                wpa = P.sb(st, "c1wpa", [128, 8, D], BF16); B_wpa = P.buf("wpa")
                wo = P.sb(st, "c1wo", [128, 8, D], BF16); B_wo = P.buf("wo")
                xf = P.sb(st, "c1xf", [128, 8, T], F32); B_xf = [P.buf("xf") for _ in range(8)]
                xb0 = P.sb(st, "c1xb", [128, 8, T], BF16)
                ot0 = P.sb(st, "c1ot", [128, 8, T], BF16)
                U = P.sb(st, "c1U", [128, 4, 528], F32); B_U = P.buf("U")
                H1 = P.sb(st, "c1H1", [128, 4, 16], F32); H2 = P.sb(st, "c1H2", [128, 4, 16], F32); B_H = P.buf("H")
                icnt = P.sb(st, "c1icnt", [128, 4, T], F32); B_ic = P.buf("icnt")
                tA = [P.sb(st, f"c1tA{i}", [128, 528], F32) for i in range(2)]
                tB = [P.sb(st, f"c1tB{i}", [128, 528], F32) for i in range(2)]
                B_tw = [P.buf("tw") for _ in range(2)]
                zt = [P.sb(st, f"c1zt{g}", [128, T], BF16) for g in range(4)]; B_zt = [P.buf("zt") for _ in range(4)]
                pm = P.sb(st, "c1pm", [128, 4, T], BF16); B_pm = [P.buf("pm") for _ in range(4)]
                sA = [P.sb(st, f"c1sA{i}", [128, T], F32) for i in range(2)]; B_sA = [P.buf("sA") for _ in range(2)]
                sB = [P.sb(st, f"c1sB{i}", [128, T], F32) for i in range(2)]; B_sB = [P.buf("sB") for _ in range(2)]
                t1 = [P.sb(st, f"c1t1{i}", [128, T], F32) for i in range(2)]; B_t1 = [P.buf("t1") for _ in range(2)]
                t2 = [P.sb(st, f"c1t2{i}", [128, T], F32) for i in range(2)]; B_t2 = [P.buf("t2") for _ in range(2)]
                mix = P.sb(st, "c1mix", [128, 8, T], BF16); B_mix = [P.buf("mix") for _ in range(8)]
                sc.dma(lambda e: e.dma_start(out=icnt[:], in_=icnt_d[:]), B_ic, writes=[B_ic])

                def yprod(fo):
                    by = PB[fo % 2]
                    py = psum[:, fo % 2, :]

                    def mm_o(e, fo=fo, py=py):
                        for kt in range(8):
                            ins = e.matmul(py, wo[:, kt, fo * 128:(fo + 1) * 128], mix[:, kt, :], start=(kt == 0), stop=(kt == 7))
                        return ins
                    sc.op("pe", mm_o, reads=[B_wo] + B_mix, writes=[by])
                    return py, by
                ln_run = residual_ln("c1", st, psum, PB, None, None, T, yprod, 1.0 / ALPHA, 16, 24)
                stw = contextlib.ExitStack()
                stages = [(P.sb(stw, f"c1stg{i}", [128, 2048], F32), P.buf("stg")) for i in range(2)]
                P.load_weight(st, win, D, 2048, wgate, B_wgt, stages, col0=3584)
                sc.dma(lambda e: e.dma_start(out=stages[0][0][:, 0:512], in_=poolw_d.rearrange("p g d -> p (g d)")),
                       stages[0][1], writes=[stages[0][1]])
                for g in range(4):
                    sc.op("dve", lambda e, g=g: e.tensor_copy(out=wpl[:, g, :], in_=stages[0][0][:, g * 128:(g + 1) * 128]),
                          reads=[stages[0][1]], writes=[B_wpl])
                P.load_weight(st, wpp_d, 512, D, wpp, B_wpp, stages)
                P.load_weight(st, wpa_d, D, D, wpa, B_wpa, stages)
                P.load_weight(st, wout_d, D, D, wo, B_wo, stages)
                stw.close()
                xb1 = P.sb(st, "c1xb1", [128, 8, T], BF16)
                ot1 = P.sb(st, "c1ot1", [128, 8, T], BF16)
                xbs, ots = [xb0, xb1], [ot0, ot1]
                B_xbs = [P.buf("xb") for _ in range(2)]
                B_ots = [P.buf("ot") for _ in range(2)]
                B_xbs[1].r = dict(sc.alltok)
                B_ots[1].r = dict(sc.alltok)
                UTv = UTs.rearrange("(g p) t -> p g t", p=128)

                def load_xo(i):
                    s = i % 2
                    tk = slice(i * T, (i + 1) * T)
                    sc.dma(lambda e: e.dma_start(out=xbs[s][:], in_=x1bT.rearrange("(kt p) t -> p kt t", p=128)[:, :, tk]),
                           B_xbs[s], reads=[B_x1bT], writes=[B_xbs[s]])
                    sc.dma(lambda e: e.dma_start(out=ots[s][:], in_=OTs.rearrange("(kt p) t -> p kt t", p=128)[:, :, tk]),
                           B_ots[s], reads=[B_OTs], writes=[B_ots[s]])

                def load_u(i):
                    tk = slice(i * T, (i + 1) * T)
                    sc.dma(lambda e: e.dma_start(out=U[:, :, 16:528], in_=UTv[:, :, tk]), B_U, reads=[B_UTs], writes=[B_U])
                    if i >= 1:
                        c0 = NOWN + 512 * (i - 1) + 496
                        sc.dma(lambda e: e.dma_start(out=H1[:], in_=UTv[:, :, c0:c0 + 16]), B_H, reads=[B_UTs], writes=[B_H])
                    else:
                        sc.op("pool", lambda e: e.memset(H1[:], 0.0), writes=[B_H])
                    c1_ = NOWN + 512 * i + 496
                    sc.dma(lambda e: e.dma_start(out=H2[:], in_=UTv[:, :, c1_:c1_ + 16]), B_H, reads=[B_UTs], writes=[B_H])

                load_xo(0)
                load_u(0)
                c1pend = None
                for i in range(NCH):
                    tk = slice(i * T, (i + 1) * T)
                    xb, ot = xbs[i % 2], ots[i % 2]
                    B_xb, B_ot = B_xbs[i % 2], B_ots[i % 2]
                    def load_xf(tk=tk):
                        sc.dma(lambda e: e.dma_start(out=xf[:], in_=x1T.rearrange("(kt p) t -> p kt t", p=128)[:, :, tk]),
                               B_xf[0], reads=[B_x1T], writes=B_xf)
                    if i == 0:
                        load_xf()
                    if i + 1 < NCH:
                        load_xo(i + 1)
                    sc.op("dve", lambda e: e.tensor_scalar(out=U[:, :, 0:16], in0=H1[:], scalar1=cols_sb[:, 72:73], scalar2=None,
                                                          op0=ALU.mult), reads=[B_H, B_cols], writes=[B_U])
                    sc.op("dve", lambda e: e.scalar_tensor_tensor(out=U[:, :, 0:16], in0=H2[:], scalar=cols_sb[:, 73:74],
                                                                  in1=U[:, :, 0:16], op0=ALU.mult, op1=ALU.add),
                          reads=[B_H, B_cols, B_U], writes=[B_U])
                    for g in range(4):
                        eng = "dve"
                        ei = g % 2
                        A_, B_ = tA[ei], tB[ei]
                        Ug = U[:, g, :]
                        bw = B_tw[ei]
                        sc.op(eng, lambda e, A_=A_, Ug=Ug: e.tensor_tensor(out=A_[:, 1:528], in0=Ug[:, 1:528], in1=Ug[:, 0:527],
                                                                         op=ALU.add), reads=[B_U], writes=[bw])
                        cur = A_
                        if g >= 1:
                            sc.op(eng, lambda e, A_=A_, B_=B_: e.tensor_tensor(out=B_[:, 3:528], in0=A_[:, 3:528], in1=A_[:, 1:526],
                                                                             op=ALU.add), reads=[bw], writes=[bw])
                            cur = B_
                        if g >= 2:
                            sc.op(eng, lambda e, A_=A_, B_=B_: e.tensor_tensor(out=A_[:, 7:528], in0=B_[:, 7:528], in1=B_[:, 3:524],
                                                                             op=ALU.add), reads=[bw], writes=[bw])
                            cur = A_
                        if g >= 3:
                            sc.op(eng, lambda e, A_=A_, B_=B_: e.tensor_tensor(out=B_[:, 15:528], in0=A_[:, 15:528], in1=A_[:, 7:520],
                                                                             op=ALU.add), reads=[bw], writes=[bw])
                            cur = B_
                        wsz = float(2 ** (g + 1))
                        if i == 0:
                            other = B_ if cur is A_ else A_
                            sc.op(eng, lambda e, cur=cur, other=other, g=g: e.tensor_tensor(out=other[:, 16:528], in0=cur[:, 16:528],
                                                                                          in1=icnt[:, g, :], op=ALU.mult),
                                  reads=[bw, B_ic], writes=[bw])
                            sc.op(eng, lambda e, other=other, g=g, Ug=Ug: e.tensor_tensor(out=zt[g][:], in0=other[:, 16:528],
                                                                                        in1=Ug[:, 16:528], op=ALU.subtract),
                                  reads=[bw, B_U], writes=[B_zt[g]])
                        else:
                            sc.op("dve", lambda e, cur=cur, g=g, Ug=Ug, wsz=wsz: e.scalar_tensor_tensor(
                                out=zt[g][:], in0=cur[:, 16:528], scalar=1.0 / wsz, in1=Ug[:, 16:528],
                                op0=ALU.mult, op1=ALU.subtract), reads=[bw, B_U], writes=[B_zt[g]])
                        sc.op("pe", lambda e, g=g: e.matmul(psum[:, g, :], wpl[:, g, :], zt[g][:], start=True, stop=True),
                              reads=[B_wpl, B_zt[g]], writes=[PB[g]])
                        sc.op("dve", lambda e, g=g: e.tensor_scalar(out=pm[:, g, :], in0=psum[:, g, :],
                                                                   scalar1=cols_sb[:, 68 + g:69 + g], scalar2=bs_sb[:, g:g + 1],
                                                                   op0=ALU.mult, op1=ALU.add),
                              reads=[PB[g], B_cols], writes=[B_pm[g]])
                    if i + 1 < NCH:
                        load_u(i + 1)
                    for ft in range(8):
                        if c1pend is not None and ft >= 1:
                            for _ in range(3):
                                if c1pend[1]:
                                    c1pend[1].pop(0)()
                            if not c1pend[1]:
                                c1pend[2]()
                                c1pend = None
                                load_xf()
                        b0 = 4 * (ft % 2)
                        q = ft % 2
                        fs = slice(ft * 128, (ft + 1) * 128)

                        def mm_p0(e, fs=fs, b0=b0, ft=ft):
                            for kt in range(4):
                                ins = e.matmul(psum[:, b0, :], wpp[:, kt, fs], pm[:, kt, :], start=(kt == 0), stop=(kt == 3))
                            return ins

                        def mm_p(e, fs=fs, b0=b0, ft=ft, xb=xb, ot=ot):
                            for kt in range(8):
                                e.matmul(psum[:, b0 + 1, :], wpa[:, kt, fs], ot[:, kt, :], start=(kt == 0), stop=(kt == 7))
                            for kt in range(8):
                                e.matmul(psum[:, b0 + 2, :], wgate[:, kt, ft * 128:(ft + 1) * 128], xb[:, kt, :],
                                         start=(kt == 0), stop=(kt == 7))
                            for kt in range(8):
                                ins = e.matmul(psum[:, b0 + 3, :], wgate[:, kt, 1024 + ft * 128:1024 + (ft + 1) * 128], xb[:, kt, :],
                                               start=(kt == 0), stop=(kt == 7))
                            return ins
                        sc.op("pe", mm_p, reads=[B_wpa, B_wgt, B_xb, B_ot], writes=PB[b0 + 1:b0 + 4])
                        sc.op("pe", mm_p0, reads=[B_wpp] + B_pm, writes=[PB[b0]])
                        sc.op("act", lambda e, q=q, b0=b0, ft=ft: e.activation(out=sA[q][:], in_=psum[:, b0 + 2, :], func=AF.Sigmoid,
                                                                              bias=cols_sb[:, 48 + ft:49 + ft], scale=1.0),
                              reads=[PB[b0 + 2], B_cols], writes=[B_sA[q]])
                        sc.op("act", lambda e, q=q, b0=b0, ft=ft: e.activation(out=sB[q][:], in_=psum[:, b0 + 3, :], func=AF.Sigmoid,
                                                                              bias=cols_sb[:, 56 + ft:57 + ft], scale=1.0),
                              reads=[PB[b0 + 3], B_cols], writes=[B_sB[q]])
                        sc.op("dve", lambda e, q=q, b0=b0: e.tensor_tensor(out=t1[q][:], in0=sA[q][:], in1=psum[:, b0, :], op=ALU.mult),
                              reads=[B_sA[q], PB[b0]], writes=[B_t1[q]])
                        sc.op("dve", lambda e, q=q, b0=b0: e.tensor_tensor(out=t2[q][:], in0=sB[q][:], in1=psum[:, b0 + 1, :],
                                                                          op=ALU.mult),
                              reads=[B_sB[q], PB[b0 + 1]], writes=[B_t2[q]])
                        sc.op("dve", lambda e, q=q, ft=ft: e.tensor_tensor(out=mix[:, ft, :], in0=t1[q][:], in1=t2[q][:], op=ALU.add),
                              reads=[B_t1[q], B_t2[q]], writes=[B_mix[ft]])
                    steps = ln_run(xf, B_xf, None, None)

                    def store(tk=tk):
                        sc.dma(lambda e: e.dma_start(out=x2T.rearrange("(kt p) t -> p kt t", p=128)[:, :, tk], in_=xf[:]),
                               B_xf[0], reads=B_xf, writes=[B_x2T])
                    c1pend = (i, steps, store)
                while c1pend[1]:
                    c1pend[1].pop(0)()
                c1pend[2]()
                sc.barrier()

        if "A1" in phases:
            ffn_phase("a1", xT, B_in, w1g, w1u, w1d, 0, 8, NTOK // 256, x1T, B_x1T, NOWN // 256, x1bT, B_x1bT)
        if "A2" in phases:
            phase_a2()
        if "B" in phases:
            phase_b()
        if "C1" in phases:
            phase_c1()
        if "C2" in phases:
            ffn_phase("c2", x2T, B_x2T, w2g, w2u, w2d, 32, 40, NOWN // 256, outT, B_out, NOWN // 256, None, None)
        sc.barrier()
        blk = P.stack.enter_context(nc.Block())
        sc.replay(blk)
    return nc


def _colmajor(v, n):
    return np.ascontiguousarray(np.asarray(v, np.float32).reshape(n, 128).T)


def token_perm(r, nch):
    own = np.concatenate([np.arange(512 * (2 * i + r), 512 * (2 * i + r + 1)) for i in range(nch)])
    oth = np.concatenate([np.arange(512 * (2 * i + 1 - r), 512 * (2 * i + 2 - r)) for i in range(nch)])
    return own, oth


def make_inputs(inp, core, nch=8):
    b, r = core // 2, core % 2
    bf = ml_dtypes.bfloat16
    own, oth = token_perm(r, nch)
    pos = np.concatenate([own, oth])
    m = {}
    m["xT"] = np.ascontiguousarray(np.asarray(inp["x"][b], np.float32)[pos].T)
    cols = np.zeros((128, 80), np.float32)
    cols[:, 0:8] = _colmajor(inp["ln1_g"][0], 8)
    cols[:, 8:16] = _colmajor(inp["ln1_b"][0], 8)
    cols[:, 16:24] = _colmajor(inp["ln2_g"][0], 8)
    cols[:, 24:32] = _colmajor(inp["ln2_b"][0], 8)
    cols[:, 32:40] = _colmajor(inp["ln3_g"][0], 8)
    cols[:, 40:48] = _colmajor(inp["ln3_b"][0], 8)
    cols[:, 48:64] = _colmajor(inp["b_gate"][0], 16)
    cols[:, 64:68] = _colmajor(np.asarray(inp["pool_b"][0]).reshape(-1), 4)
    cols[:, 68:72] = _colmajor(inp["pool_scale"][0], 4)
    cols[:, 72] = 1.0 - r
    cols[:, 73] = float(r)
    m["cols"] = cols
    m["ones"] = np.full((128, 128), 1.0 / 1024.0, np.float32).astype(bf)
    m["ident"] = np.eye(128, dtype=np.float32).astype(bf)
    kk, qq = np.meshgrid(np.arange(128), np.arange(128), indexing="ij")
    m["tri"] = np.where(kk > qq, NEG, 0.0).astype(np.float32).astype(bf)
    m["omask"] = np.full((128, 512), NEG if r == 0 else 0.0, np.float32).astype(bf)
    slopes = 2.0 ** (-(np.arange(8) + 1.0))
    kc = np.zeros((8, 5, pos.size), np.float32)
    qc = np.zeros((8, 5, own.size), np.float32)
    for h in range(8):
        kc[h, 0] = slopes[h] * (pos % 128)
        kc[h, 1] = slopes[h] * 128.0 * (pos // 128)
        kc[h, 2:5] = 1.0
        ql = own % 512
        qc[h, 0:2] = 1.0
        qc[h, 2] = -slopes[h] * (ql & 255)
        qc[h, 3] = -slopes[h] * (ql & 256)
        qc[h, 4] = -slopes[h] * (own - ql)
    m["kconst"] = kc.astype(bf)
    nkb = pos.size // 128
    kpos = pos.reshape(nkb, 128).T.astype(np.float64)
    qend = own.reshape(nch, 512)[:, 511].astype(np.float64)
    bt = np.zeros((8, 128, nkb * nch), np.float32)
    blkmask = np.zeros((nkb, nch), np.float64)
    if r == 0:
        for i in range(nch):
            j0 = (own.size + 512 * i) // 128
            blkmask[j0:j0 + 4, i] = NEG
    for h in range(8):
        bt[h] = (slopes[h] * (kpos[:, :, None] - qend[None, None, :]) + blkmask[None]).reshape(128, nkb * nch)
    m["btab"] = bt
    m["qconst"] = qc.astype(bf)
    ic = np.zeros((128, 4, 512), np.float32)
    t0 = own[:512]
    for g in range(4):
        w = 2 ** (g + 1)
        ic[:, g, :] = (1.0 / np.minimum(t0 + 1, w))[None, :]
    m["icnt"] = ic
    lv = np.stack([inp["lambda_q1"][0], inp["lambda_k1"][0], inp["lambda_q2"][0], inp["lambda_k2"][0]]).astype(np.float32)
    m["lamv"] = np.ascontiguousarray(np.broadcast_to(lv[None], (128, 4, 64)))
    cols[:, 74] = np.asarray(inp["subln_g"][0], np.float32)
    m["ones1"] = np.ones((128, 128), np.float32).astype(bf)
    m["ones128"] = np.full((128, 128), 1.0 / 128.0, np.float32).astype(bf)
    m["w1g"] = np.asarray(inp["ffn1_w_gate"][0], np.float32)
    m["w1u"] = np.asarray(inp["ffn1_w_up"][0], np.float32)
    m["w1d"] = np.asarray(inp["ffn1_w_down"][0], np.float32)
    m["w2g"] = np.asarray(inp["ffn2_w_gate"][0], np.float32)
    m["w2u"] = np.asarray(inp["ffn2_w_up"][0], np.float32)
    m["w2d"] = np.asarray(inp["ffn2_w_down"][0], np.float32)
    m["win"] = np.asarray(inp["w_in"][0], np.float32)
    m["poolw"] = np.ascontiguousarray(np.asarray(inp["pool_w"][0], np.float32).transpose(1, 0, 2))
    m["wpp"] = np.asarray(inp["w_proj_pool"][0], np.float32)
    m["wpa"] = np.asarray(inp["w_proj_attn"][0], np.float32)
    m["wout"] = np.asarray(inp["w_out"][0], np.float32)
    return m


_NC_CACHE = {}


def kernel(**inputs):
    if "nc" not in _NC_CACHE:
        _NC_CACHE["nc"] = build({})
    nc = _NC_CACHE["nc"]
    in_maps = [make_inputs(inputs, c) for c in range(8)]
    res = run_bass_kernel_spmd(nc, in_maps, core_ids=list(range(8)))
    out = np.zeros((4, S, D), np.float32)
    for c in range(8):
        b, r = c // 2, c % 2
        own, _ = token_perm(r, 8)
        out[b, own, :] = np.asarray(res.results[c]["outT"], np.float32).T
    return out
```

```python
import contextlib
import math
import numpy as np
import ml_dtypes
import concourse.bass as bass
import concourse.mybir as mybir
from concourse.bass_utils import run_bass_kernel_spmd

F32 = mybir.dt.float32
BF16 = mybir.dt.bfloat16
AF = mybir.ActivationFunctionType
ALU = mybir.AluOpType
AX = mybir.AxisListType

D = 1024
S = 8192
DFF = 2816
NFT = DFF // 128
DIN = 5632
ALPHA = 2.0 ** 0.25
LN_EPS = 1e-5
RMS_EPS = 1e-5
LAM_INIT = 0.8 - 0.6 * math.exp(-0.3 * 0)
NEG = -30000.0
SEM_LIMIT = 20000


class Buf:
    __slots__ = ("name", "w", "r", "dsem", "dcnt")

    def __init__(self, name):
        self.name = name
        self.w = {}
        self.r = {}
        self.dsem = None
        self.dcnt = 0


def _merge(dst, src):
    for k, v in src.items():
        if dst.get(k, 0) < v:
            dst[k] = v


class Stream:
    def __init__(self, name):
        self.name = name
        self.ops = []
        self.waited = {}
        self.sem = None
        self.cnt = 0
        self.nsem = 0
        self.own = set()


class Sched:
    def __init__(self, nc, stack):
        self.nc = nc
        self.stack = stack
        self.streams = {n: Stream(n) for n in ("sync", "act", "dve", "pool", "pe")}
        self.alltok = {}

    def new_sem(self, name):
        return self.stack.enter_context(self.nc.semaphore(name))

    def _waits(self, st, deps):
        for sem, val in deps.items():
            if st.waited.get(sem, 0) < val:
                st.waited[sem] = val
                st.ops.append(("wait", sem, val))

    def _deps(self, reads, writes):
        deps = {}
        for b in reads:
            _merge(deps, b.w)
        for b in writes:
            _merge(deps, b.w)
            _merge(deps, b.r)
        return deps

    def _commit(self, tok, reads, writes):
        for b in reads:
            _merge(b.r, tok)
        for b in writes:
            _merge(b.w, tok)
            b.r = {}
        _merge(self.alltok, tok)

    def op(self, sname, fn, reads=(), writes=()):
        st = self.streams[sname]
        deps = self._deps(reads, writes)
        if sname == "pe":
            deps = {k: v for k, v in deps.items() if k not in st.own}
        self._waits(st, deps)
        if st.sem is None or st.cnt >= SEM_LIMIT:
            st.sem = self.new_sem(f"s_{sname}_{st.nsem}")
            st.own.add(st.sem)
            st.nsem += 1
            st.cnt = 0
        st.cnt += 1
        tok = {st.sem: st.cnt}
        st.ops.append(("op", fn, st.sem, 1))
        self._commit(tok, reads, writes)

    def dma(self, fn, sb, reads=(), writes=(), sname="sync"):
        st = self.streams[sname]
        self._waits(st, self._deps(reads, writes))
        if sb.dsem is None or sb.dcnt >= SEM_LIMIT:
            sb.dsem = self.new_sem(f"d_{sb.name}_{sb.dcnt}")
            sb.dcnt = 0
        sb.dcnt += 16
        tok = {sb.dsem: sb.dcnt}
        st.ops.append(("op", fn, sb.dsem, 16))
        self._commit(tok, reads, writes)

    def barrier(self):
        for st in self.streams.values():
            self._waits(st, self.alltok)

    def replay(self, block):
        def mk(st):
            def run(eng):
                for o in st.ops:
                    if o[0] == "wait":
                        eng.wait_ge(o[1], o[2])
                    else:
                        o[1](eng).then_inc(o[2], o[3])
            return run
        block.sync(mk(self.streams["sync"]))
        block.scalar(mk(self.streams["act"]))
        block.vector(mk(self.streams["dve"]))
        block.gpsimd(mk(self.streams["pool"]))
        block.tensor(mk(self.streams["pe"]))


class Prog:
    def __init__(self, cfg):
        self.cfg = cfg
        self.nc = bass.Bass("TRN2", target_bir_lowering=False)
        self.stack = contextlib.ExitStack()
        self.sc = Sched(self.nc, self.stack)
        self.nbuf = 0

    def dram_in(self, name, shape, dt=F32):
        return self.nc.dram_tensor(name, list(shape), dt, kind="ExternalInput").ap()

    def dram_out(self, name, shape, dt=F32):
        return self.nc.dram_tensor(name, list(shape), dt, kind="ExternalOutput").ap()

    def dram_tmp(self, name, shape, dt):
        kind = "ExternalOutput" if name in self.cfg.get("debug_out", ()) else "Internal"
        return self.nc.dram_tensor(name, list(shape), dt, kind=kind).ap()

    def sb(self, st, name, shape, dt):
        return st.enter_context(self.nc.sbuf_tensor("sb_" + name, list(shape), dt))

    def buf(self, name):
        self.nbuf += 1
        return Buf(f"{name}{self.nbuf}")

    def load_weight(self, st, w_ap, K, F, wb, wbuf, stages, f_chunk=2048, col0=0):
        sc = self.sc
        wv = w_ap.rearrange("(kt p) f -> p kt f", p=128)
        i = self._wl_i if hasattr(self, "_wl_i") else 0
        for kt in range(K // 128):
            for f0 in range(0, F, f_chunk):
                f1 = min(F, f0 + f_chunk)
                stg, sbuf_ = stages[i % len(stages)]
                sc.dma(lambda e, o=stg[:, 0:f1 - f0], s=wv[:, kt, col0 + f0:col0 + f1]: e.dma_start(out=o, in_=s),
                       sbuf_, writes=[sbuf_])
                if i % 2 == 0:
                    sc.op("dve", lambda e, o=wb[:, kt, f0:f1], s=stg[:, 0:f1 - f0]: e.tensor_copy(out=o, in_=s),
                          reads=[sbuf_], writes=[wbuf])
                else:
                    sc.op("act", lambda e, o=wb[:, kt, f0:f1], s=stg[:, 0:f1 - f0]: e.copy(out=o, in_=s),
                          reads=[sbuf_], writes=[wbuf])
                i += 1
        self._wl_i = i

    def load_weight_cols(self, w_ap, K, f0, f1, wb, wbuf, stages):
        sc = self.sc
        wv = w_ap.rearrange("(kt p) f -> p kt f", p=128)
        i = self._wl_i if hasattr(self, "_wl_i") else 0
        for kt in range(K // 128):
            stg, sbuf_ = stages[i % len(stages)]
            sc.dma(lambda e, o=stg[:, 0:f1 - f0], s=wv[:, kt, f0:f1]: e.dma_start(out=o, in_=s), sbuf_, writes=[sbuf_])
            if i % 2 == 0:
                sc.op("dve", lambda e, o=wb[:, kt, f0:f1], s=stg[:, 0:f1 - f0]: e.tensor_copy(out=o, in_=s),
                      reads=[sbuf_], writes=[wbuf])
            else:
                sc.op("act", lambda e, o=wb[:, kt, f0:f1], s=stg[:, 0:f1 - f0]: e.copy(out=o, in_=s),
                      reads=[sbuf_], writes=[wbuf])
            i += 1
        self._wl_i = i


def build(cfg):
    P = Prog(cfg)
    nc, sc = P.nc, P.sc
    NCH = cfg.get("n_chunks", 8)
    NOWN = NCH * 512
    NTOK = 2 * NOWN
    NH = cfg.get("n_heads", 8)
    phases = cfg.get("phases", ("A1", "A2", "B", "C1", "C2"))

    xT = P.dram_in("xT", [D, NTOK])
    cols = P.dram_in("cols", [128, 80])
    ones_d = P.dram_in("ones", [128, 128], BF16)
    ident_d = P.dram_in("ident", [128, 128], BF16)
    tri_d = P.dram_in("tri", [128, 128], BF16)
    omask_d = P.dram_in("omask", [128, 512], BF16)
    kconst = P.dram_in("kconst", [8, 5, NTOK], BF16)
    qconst = P.dram_in("qconst", [8, 5, NOWN], BF16)
    btab_d = P.dram_in("btab", [8, 128, (NTOK // 128) * NCH])
    icnt_d = P.dram_in("icnt", [128, 4, 512])
    lamv_d = P.dram_in("lamv", [128, 4, 64])
    ones1_d = P.dram_in("ones1", [128, 128], BF16)
    ones128_d = P.dram_in("ones128", [128, 128], BF16)
    w1g = P.dram_in("w1g", [D, DFF]); w1u = P.dram_in("w1u", [D, DFF]); w1d = P.dram_in("w1d", [DFF, D])
    w2g = P.dram_in("w2g", [D, DFF]); w2u = P.dram_in("w2u", [D, DFF]); w2d = P.dram_in("w2d", [DFF, D])
    win = P.dram_in("win", [D, DIN])
    poolw_d = P.dram_in("poolw", [128, 4, 128])
    wpp_d = P.dram_in("wpp", [512, D]); wpa_d = P.dram_in("wpa", [D, D]); wout_d = P.dram_in("wout", [D, D])
    outT = P.dram_out("outT", [D, NOWN])
    x1T = P.dram_tmp("x1T", [D, NOWN], F32)
    x1bT = P.dram_tmp("x1bT", [D, NTOK], BF16)
    KTs = P.dram_tmp("KTs", [1024, NTOK], BF16)
    Vs = P.dram_tmp("Vs", [NTOK, 1024], BF16)
    UTs = P.dram_tmp("UTs", [512, NTOK], F32)
    QTs = P.dram_tmp("QTs", [1024, NOWN], BF16)
    OTs = P.dram_tmp("OTs", [1024, NOWN], BF16)
    x2T = P.dram_tmp("x2T", [D, NOWN], F32)
    B_in = P.buf("inputs")
    B_x1T = P.buf("x1T"); B_x1bT = P.buf("x1bT"); B_KTs = P.buf("KTs"); B_Vs = P.buf("Vs"); B_UTs = P.buf("UTs")
    B_QTs = P.buf("QTs"); B_OTs = P.buf("OTs"); B_x2T = P.buf("x2T"); B_out = P.buf("outT")

    with P.stack:
        st0 = P.stack
        cols_sb = P.sb(st0, "cols_sb", [128, 80], F32); B_cols = P.buf("cols")
        ones_sb = P.sb(st0, "ones_sb", [128, 128], BF16); B_ones = P.buf("ones")
        ident_sb = P.sb(st0, "ident_sb", [128, 128], BF16)
        tri_sb = P.sb(st0, "tri_sb", [128, 128], BF16)
        omask_sb = P.sb(st0, "omask_sb", [128, 512], BF16)
        B_cst = P.buf("cst")
        lamv = P.sb(st0, "lamv", [128, 4, 64], F32)
        ones1_sb = P.sb(st0, "ones1_sb", [128, 128], BF16)
        ones128_sb = P.sb(st0, "ones128_sb", [128, 128], BF16)
        ltmp = P.sb(st0, "ltmp", [128, 2, 64], F32)
        lsc = P.sb(st0, "lsc", [128, 8], F32)
        bs_sb = P.sb(st0, "bs_sb", [128, 4], F32)
        B_lam = P.buf("lam")
        sc.dma(lambda e: e.dma_start(out=cols_sb[:], in_=cols[:]), B_cols, writes=[B_cols])
        sc.dma(lambda e: e.dma_start(out=ones_sb[:], in_=ones_d[:]), B_ones, writes=[B_ones])
        sc.dma(lambda e: e.dma_start(out=ident_sb[:], in_=ident_d[:]), B_cst, writes=[B_cst])
        sc.dma(lambda e: e.dma_start(out=tri_sb[:], in_=tri_d[:]), B_cst, writes=[B_cst])
        sc.dma(lambda e: e.dma_start(out=omask_sb[:], in_=omask_d[:]), B_cst, writes=[B_cst])
        sc.dma(lambda e: e.dma_start(out=lamv[:], in_=lamv_d[:]), B_lam, writes=[B_lam])
        sc.dma(lambda e: e.dma_start(out=ones1_sb[:], in_=ones1_d[:]), B_cst, writes=[B_cst])
        sc.dma(lambda e: e.dma_start(out=ones128_sb[:], in_=ones128_d[:]), B_cst, writes=[B_cst])
        sc.op("dve", lambda e: e.tensor_tensor(out=ltmp[:, 0, :], in0=lamv[:, 0, :], in1=lamv[:, 1, :], op=ALU.mult),
              reads=[B_lam], writes=[B_lam])
        sc.op("dve", lambda e: e.tensor_tensor(out=ltmp[:, 1, :], in0=lamv[:, 2, :], in1=lamv[:, 3, :], op=ALU.mult),
              reads=[B_lam], writes=[B_lam])
        sc.op("dve", lambda e: e.reduce_sum(out=lsc[:, 0:1], in_=ltmp[:, 0, :], axis=AX.X), reads=[B_lam], writes=[B_lam])
        sc.op("dve", lambda e: e.reduce_sum(out=lsc[:, 1:2], in_=ltmp[:, 1, :], axis=AX.X), reads=[B_lam], writes=[B_lam])
        sc.op("act", lambda e: e.activation(out=lsc[:, 2:4], in_=lsc[:, 0:2], func=AF.Exp), reads=[B_lam], writes=[B_lam])
        sc.op("dve", lambda e: e.tensor_tensor(out=lsc[:, 4:5], in0=lsc[:, 2:3], in1=lsc[:, 3:4], op=ALU.subtract),
              reads=[B_lam], writes=[B_lam])
        sc.op("dve", lambda e: e.tensor_scalar(out=lsc[:, 5:6], in0=lsc[:, 4:5], scalar1=LAM_INIT, scalar2=-1.0,
                                              op0=ALU.add, op1=ALU.mult), reads=[B_lam], writes=[B_lam])
        sc.op("dve", lambda e: e.tensor_scalar(out=lsc[:, 6:7], in0=cols_sb[:, 74:75], scalar1=1.0 - LAM_INIT, scalar2=None,
                                              op0=ALU.mult), reads=[B_lam, B_cols], writes=[B_lam])
        sc.op("dve", lambda e: e.tensor_tensor(out=bs_sb[:], in0=cols_sb[:, 64:68], in1=cols_sb[:, 68:72], op=ALU.mult),
              reads=[B_cols], writes=[B_cols])
        neglam = lsc[:, 5:6]
        gcolv = lsc[:, 6:7]

        def residual_ln(pfx, st, psum, PB, xfs, B_xs, T, yprod, c_res, gcol, bcol, obs=None, B_obs=None):
            zb = [P.sb(st, f"{pfx}zb{i}", [128, T], BF16) for i in range(2)]
            B_zb = [P.buf("zb") for _ in range(2)]
            zq = [P.sb(st, f"{pfx}zq{i}", [128, T], BF16) for i in range(2)]
            B_zq = [P.buf("zq") for _ in range(2)]
            mean = P.sb(st, pfx + "mean", [128, T], F32); B_mean = P.buf("mean")
            sd = P.sb(st, pfx + "sd", [128, T], F32); B_sd = P.buf("sd")
            msq = P.sb(st, pfx + "msq", [128, T], F32); B_msq = P.buf("msq")
            eps_eff = LN_EPS / (ALPHA * ALPHA)

            def run(xfs, B_xs, obs, B_obs, hook=None):
                def stats(fo):
                    q = fo % 2

                    def mm_st(e, q=q, fo=fo):
                        e.matmul(psum[:, 6, 0:T], ones_sb[:], zb[q][:], start=(fo == 0), stop=(fo == 7))
                        return e.matmul(psum[:, 7, 0:T], ones_sb[:], zq[q][:], start=(fo == 0), stop=(fo == 7))
                    sc.op("pe", mm_st, reads=[B_ones, B_zb[q], B_zq[q]], writes=[PB[6], PB[7]])
                for fo in range(8):
                    py, by = yprod(fo)
                    if fo >= 1:
                        stats(fo - 1)
                    zs = xfs[:, fo, :]
                    sc.op("dve", lambda e, zs=zs, py=py: e.scalar_tensor_tensor(out=zs, in0=py, scalar=c_res, in1=zs,
                                                                                 op0=ALU.mult, op1=ALU.add),
                          reads=[by, B_xs[fo]], writes=[B_xs[fo]])
                    q = fo % 2
                    sc.op("dve", lambda e, q=q, zs=zs: e.tensor_copy(out=zb[q][:], in_=zs), reads=[B_xs[fo]], writes=[B_zb[q]])
                    sc.op("dve", lambda e, q=q, zs=zs: e.tensor_tensor(out=zq[q][:], in0=zs, in1=zs, op=ALU.mult),
                          reads=[B_xs[fo]], writes=[B_zq[q]])
                    if fo == 2 and hook is not None:
                        hook()
                stats(7)

                def st_stats():
                    sc.op("dve", lambda e: e.tensor_copy(out=mean[:], in_=psum[:, 6, 0:T]), reads=[PB[6]], writes=[B_mean])
                    sc.op("dve", lambda e: e.tensor_tensor(out=msq[:], in0=mean[:], in1=mean[:], op=ALU.mult),
                          reads=[B_mean], writes=[B_msq])
                    sc.op("dve", lambda e: e.tensor_tensor(out=msq[:], in0=psum[:, 7, 0:T], in1=msq[:], op=ALU.subtract),
                          reads=[PB[7], B_msq], writes=[B_msq])
                    sc.op("act", lambda e: e.activation(out=sd[:], in_=msq[:], func=AF.Sqrt, bias=eps_eff, scale=1.0),
                          reads=[B_msq], writes=[B_sd])

                def st_recip():
                    sc.op("dve", lambda e: e.reciprocal(out=sd[:], in_=sd[:]), reads=[B_sd], writes=[B_sd])

                def st_norm(which, half):
                    def f():
                        ks = slice(4 * half, 4 * half + 4)
                        xa = xfs[:, ks, :]
                        src = mean if which == 0 else sd
                        bsrc = B_mean if which == 0 else B_sd
                        op = ALU.subtract if which == 0 else ALU.mult
                        sc.op("dve", lambda e: e.tensor_tensor(out=xa, in0=xa, in1=src[:].unsqueeze(1).broadcast_to([128, 4, T]),
                                                               op=op), reads=list(B_xs[ks]) + [bsrc], writes=list(B_xs[ks]))
                    return f

                def st_aff(fo):
                    def f():
                        zs = xfs[:, fo, :]
                        sc.op("act", lambda e: e.activation(out=zs, in_=zs, func=AF.Identity,
                                                            bias=cols_sb[:, bcol + fo:bcol + fo + 1],
                                                            scale=cols_sb[:, gcol + fo:gcol + fo + 1]),
                              reads=[B_xs[fo], B_cols], writes=[B_xs[fo]])
                        if obs is not None:
                            sc.op("act", lambda e: e.copy(out=obs[:, fo, :], in_=zs), reads=[B_xs[fo]], writes=[B_obs[fo]])
                    return f
                steps = [st_stats, st_recip, st_norm(0, 0), st_norm(0, 1), st_norm(1, 0), st_norm(1, 1)]
                steps += [st_aff(fo) for fo in range(8)]
                return steps
            return run

        def ffn_phase(name, src, B_src, wg_d, wu_d, wd_d, gcol, bcol, ntiles, dst_f32, B_dst32, n32, dst_bf16, B_dst16):
            T = 256
            with contextlib.ExitStack() as st:
                psum = st.enter_context(nc.psum_tensor(name + "ps", [128, 8, 512], F32))
                PB = [P.buf(f"psum{i}") for i in range(8)]
                wg = P.sb(st, name + "wg", [128, 8, DFF], BF16); B_wg = P.buf("wg")
                wu = P.sb(st, name + "wu", [128, 8, DFF], BF16); B_wu = P.buf("wu")
                wd = P.sb(st, name + "wd", [128, NFT, D], BF16); B_wd = P.buf("wd")
                stages = [(P.sb(st, f"{name}stg{i}", [128, 2048], F32), P.buf("stg")) for i in range(2)]
                xf = [P.sb(st, f"{name}xf{i}", [128, 8, T], F32) for i in range(2)]
                B_xf = [[P.buf("xf") for _ in range(8)] for _ in range(2)]
                xb = [P.sb(st, f"{name}xb{i}", [128, 8, T], BF16) for i in range(2)]
                B_xb = [P.buf("xb") for _ in range(2)]
                ob = [P.sb(st, f"{name}ob{i}", [128, 8, T], BF16) for i in range(2)] if dst_bf16 is not None else [None, None]
                B_ob = [[P.buf("ob") for _ in range(8)] for _ in range(2)]
                hT = P.sb(st, name + "hT", [128, NFT, T], BF16)
                B_h = [P.buf("h") for _ in range(NFT)]
                sg = [P.sb(st, f"{name}sg{i}", [128, T], F32) for i in range(4)]
                B_sg = [P.buf("sg") for _ in range(4)]
                srcv = src.rearrange("(kt p) t -> p kt t", p=128)

                def load_x(t):
                    s = t % 2
                    sc.dma(lambda e, s=s, t=t: e.dma_start(out=xf[s][:], in_=srcv[:, :, t * T:(t + 1) * T]), B_xb[s],
                           reads=[B_src], writes=B_xf[s])

                def cast_x(t):
                    s = t % 2
                    sc.op("act", lambda e, s=s: e.copy(out=xb[s][:], in_=xf[s][:]), reads=B_xf[s], writes=[B_xb[s]])

                load_x(0)
                B_wgc = [P.buf("wgc") for _ in range(2)]
                B_wuc = [P.buf("wuc") for _ in range(2)]
                P.load_weight_cols(wg_d, D, 0, 2048, wg, B_wgc[0], stages)
                cast_x(0)
                if ntiles > 1:
                    load_x(1)
                P.load_weight_cols(wu_d, D, 0, 2048, wu, B_wuc[0], stages)
                P.load_weight_cols(wg_d, D, 2048, DFF, wg, B_wgc[1], stages)
                P.load_weight_cols(wu_d, D, 2048, DFF, wu, B_wuc[1], stages)
                P.load_weight(st, wd_d, DFF, D, wd, B_wd, stages)

                def yprod(fo):
                    by = PB[4 + fo % 2]
                    py = psum[:, 4 + fo % 2, 0:T]

                    def mm_d(e, fo=fo, py=py):
                        for kt in range(NFT):
                            ins = e.matmul(py, wd[:, kt, fo * 128:(fo + 1) * 128], hT[:, kt, :],
                                           start=(kt == 0), stop=(kt == NFT - 1))
                        return ins
                    sc.op("pe", mm_d, reads=[B_wd] + B_h, writes=[by])
                    return py, by
                ln_run = residual_ln(name, st, psum, PB, None, None, T, yprod, 0.5 / ALPHA, gcol, bcol)

                def finish(t):
                    s = t % 2
                    if dst_f32 is not None and t < n32:
                        dv = dst_f32.rearrange("(kt p) t -> p kt t", p=128)[:, :, t * T:(t + 1) * T]
                        sc.dma(lambda e, dv=dv, s=s: e.dma_start(out=dv, in_=xf[s][:]), B_xb[s], reads=B_xf[s], writes=[B_dst32])
                    if dst_bf16 is not None:
                        dv16 = dst_bf16.rearrange("(kt p) t -> p kt t", p=128)[:, :, t * T:(t + 1) * T]
                        sc.dma(lambda e, dv16=dv16, s=s: e.dma_start(out=dv16, in_=ob[s][:]), B_ob[s][0], reads=B_ob[s],
                               writes=[B_dst16])

                pending = None
                for t in range(ntiles):
                    s = t % 2
                    for ft in range(NFT):
                        if ft >= 4 and pending is not None:
                            if pending[1]:
                                pending[1].pop(0)()
                            else:
                                finish(pending[0])
                                pending = None
                                if t + 1 < ntiles:
                                    load_x(t + 1)
                        i0 = ft % 4
                        bg = bu = PB[i0]
                        pg = psum[:, i0, 0:T]
                        pu = psum[:, i0, T:2 * T]

                        def mm_gu(e, ft=ft, pg=pg, pu=pu, s=s):
                            for kt in range(8):
                                e.matmul(pg, wg[:, kt, ft * 128:(ft + 1) * 128], xb[s][:, kt, :],
                                         start=(kt == 0), stop=(kt == 7))
                            for kt in range(8):
                                ins = e.matmul(pu, wu[:, kt, ft * 128:(ft + 1) * 128], xb[s][:, kt, :],
                                               start=(kt == 0), stop=(kt == 7))
                            return ins
                        fc = 0 if ft < 16 else 1
                        sc.op("pe", mm_gu, reads=[B_wgc[fc], B_wuc[fc], B_xb[s]], writes=[bg])
                        q = ft % 4
                        sc.op("act", lambda e, q=q, pg=pg: e.activation(out=sg[q][:], in_=pg, func=AF.Silu),
                              reads=[bg], writes=[B_sg[q]])
                        sc.op("dve", lambda e, q=q, pu=pu, ft=ft: e.tensor_tensor(out=hT[:, ft, :], in0=sg[q][:], in1=pu,
                                                                                 op=ALU.mult),
                              reads=[B_sg[q], bu], writes=[B_h[ft]])
                    if pending is not None:
                        while pending[1]:
                            pending[1].pop(0)()
                        finish(pending[0])
                        pending = None
                        if t + 1 < ntiles:
                            load_x(t + 1)
                    hook = (lambda t=t: cast_x(t + 1)) if t + 1 < ntiles else None
                    steps = ln_run(xf[s], B_xf[s], ob[s] if (dst_bf16 is not None) else None, B_ob[s], hook)
                    pending = (t, steps)
                while pending[1]:
                    pending[1].pop(0)()
                finish(pending[0])
                sc.barrier()

        def phase_a2():
            T = 512
            with contextlib.ExitStack() as st:
                psum = st.enter_context(nc.psum_tensor("a2ps", [128, 8, 512], F32))
                PB = [P.buf(f"psum{i}") for i in range(8)]
                w = P.sb(st, "a2w", [128, 8, 3584], BF16); B_w = P.buf("a2w")
                stages = [(P.sb(st, f"a2stg{i}", [128, 2048], F32), P.buf("stg")) for i in range(2)]
                xb = [P.sb(st, f"a2xb{i}", [128, 8, T], BF16) for i in range(2)]
                B_xb = [P.buf("xb") for _ in range(2)]
                kst = [P.sb(st, f"a2k{i}", [128, 8, T], BF16) for i in range(2)]
                B_k = [[P.buf("k") for _ in range(8)] for _ in range(2)]
                qst = [P.sb(st, f"a2q{i}", [128, 8, T], BF16) for i in range(2)]
                B_q = [[P.buf("q") for _ in range(8)] for _ in range(2)]
                ust = [P.sb(st, f"a2u{i}", [128, 4, T], F32) for i in range(2)]
                B_u = [[P.buf("u") for _ in range(4)] for _ in range(2)]
                vst = [P.sb(st, f"a2v{i}", [128, 4, 1024], BF16) for i in range(2)]
                B_v = [[P.buf("v") for _ in range(8)] for _ in range(2)]
                srcv = x1bT.rearrange("(kt p) t -> p kt t", p=128)
                ntiles = NTOK // T

                def load_x(t):
                    s = t % 2
                    sc.dma(lambda e, s=s, t=t: e.dma_start(out=xb[s][:], in_=srcv[:, :, t * T:(t + 1) * T]), B_xb[s],
                           reads=[B_x1bT], writes=[B_xb[s]])
                load_x(0)
                B_wc = [P.buf("a2wc") for _ in range(2)]
                P.load_weight_cols(win, D, 0, 2048, w, B_wc[0], stages)
                P.load_weight_cols(win, D, 2048, 3584, w, B_wc[1], stages)
                cnt = [0]

                def proj(s, col, dst_ap, B_dst, scale=None, c0=0):
                    k = cnt[0] % 8
                    cnt[0] += 1
                    pp = psum[:, k, c0:T]
                    dst_ap = dst_ap[:, c0:T]

                    def mm(e, col=col, pp=pp, s=s):
                        for kt in range(8):
                            ins = e.matmul(pp, w[:, kt, col:col + 128], xb[s][:, kt, c0:T], start=(kt == 0), stop=(kt == 7))
                        return ins
                    sc.op("pe", mm, reads=[B_wc[0 if col < 2048 else 1], B_xb[s]], writes=[PB[k]])
                    if k % 2 == 0:
                        if scale is None:
                            fn = lambda e: e.copy(out=dst_ap, in_=pp)
                        else:
                            fn = lambda e: e.mul(out=dst_ap, in_=pp, mul=scale)
                        sc.op("act", fn, reads=[PB[k]], writes=[B_dst])
                    else:
                        if scale is None:
                            fn = lambda e: e.tensor_copy(out=dst_ap, in_=pp)
                        else:
                            fn = lambda e: e.tensor_scalar(out=dst_ap, in0=pp, scalar1=scale, scalar2=None, op0=ALU.mult)
                        sc.op("dve", fn, reads=[PB[k]], writes=[B_dst])

                for t in range(ntiles):
                    s = t % 2
                    if t + 1 < ntiles:
                        load_x(t + 1)
                    tk = slice(t * T, (t + 1) * T)
                    for ft in range(4):
                        if t < NOWN // T:
                            proj(s, ft * 128, ust[s][:, ft, :], B_u[s][ft])
                        else:
                            proj(s, ft * 128, ust[s][:, ft, :], B_u[s][ft], c0=T - 16)
                    sc.dma(lambda e, s=s, tk=tk: e.dma_start(out=UTs.rearrange("(g p) t -> p g t", p=128)[:, :, tk], in_=ust[s][:]),
                           B_u[s][0], reads=B_u[s], writes=[B_UTs])
                    if t < NOWN // T:
                        for ft in range(8):
                            proj(s, 512 + ft * 128, qst[s][:, ft, :], B_q[s][ft], scale=0.125)
                        sc.dma(lambda e, s=s, tk=tk: e.dma_start(out=QTs.rearrange("(g p) t -> p g t", p=128)[:, :, tk], in_=qst[s][:]),
                               B_q[s][0], reads=B_q[s], writes=[B_QTs])
                    for ft in range(8):
                        proj(s, 1536 + ft * 128, kst[s][:, ft, :], B_k[s][ft])
                    sc.dma(lambda e, s=s, tk=tk: e.dma_start(out=KTs.rearrange("(g p) t -> p g t", p=128)[:, :, tk], in_=kst[s][:]),
                           B_k[s][0], reads=B_k[s], writes=[B_KTs])
                    for ts in range(4):
                        for fc in range(2):
                            k = cnt[0] % 8
                            cnt[0] += 1
                            pp = psum[:, k, :]

                            def mmv(e, ts=ts, fc=fc, pp=pp, s=s):
                                for kt in range(8):
                                    ins = e.matmul(pp, xb[s][:, kt, ts * 128:(ts + 1) * 128],
                                                   w[:, kt, 2560 + fc * 512:2560 + (fc + 1) * 512],
                                                   start=(kt == 0), stop=(kt == 7))
                                return ins
                            sc.op("pe", mmv, reads=[B_wc[1], B_xb[s]], writes=[PB[k]])
                            dst_ap = vst[s][:, ts, fc * 512:(fc + 1) * 512]
                            if k % 2 == 0:
                                sc.op("act", lambda e, dst_ap=dst_ap, pp=pp: e.copy(out=dst_ap, in_=pp), reads=[PB[k]],
                                      writes=[B_v[s][ts * 2 + fc]])
                            else:
                                sc.op("dve", lambda e, dst_ap=dst_ap, pp=pp: e.tensor_copy(out=dst_ap, in_=pp), reads=[PB[k]],
                                      writes=[B_v[s][ts * 2 + fc]])
                    sc.dma(lambda e, s=s, t=t: e.dma_start(out=Vs[t * T:(t + 1) * T, :].rearrange("(ts p) f -> p ts f", p=128),
                                                           in_=vst[s][:]),
                           B_v[s][0], reads=B_v[s], writes=[B_Vs])
                sc.barrier()

        def phase_b():
            NKB = NTOK // 128
            PACK_FROM = cfg.get("pack_from", 2)
            with contextlib.ExitStack() as st:
                psum = st.enter_context(nc.psum_tensor("bps", [128, 8, 512], F32))
                PS2 = [P.buf("S2a"), P.buf("S2b")]
                B_O = P.buf("O"); B_L0 = P.buf("L0"); B_L1 = P.buf("L1")
                KT = [[P.sb(st, f"bK{sl}{m}", [128, NTOK], BF16) for m in range(2)] for sl in range(2)]
                QT = [[P.sb(st, f"bQ{sl}{m}", [128, NOWN], BF16) for m in range(2)] for sl in range(2)]
                VA = [P.sb(st, f"bV{sl}", [128, NKB, 128], BF16) for sl in range(2)]
                btab = [P.sb(st, f"bbt{sl}", [128, NKB * NCH], F32) for sl in range(2)]
                B_hd = [P.buf("hd") for _ in range(2)]
                NPT = 6
                PT = [P.sb(st, f"bP{i}", [128, 2, 512], BF16) for i in range(NPT)]
                B_PT = [P.buf("PT") for _ in range(NPT)]
                Osb = P.sb(st, "bOsb", [128, 2, 512], F32); B_Osb = P.buf("Osb")
                L1sb = P.sb(st, "bL1sb", [128, 512], F32); B_L1sb = P.buf("L1sb")
                rL = P.sb(st, "brL", [128, 2, 512], F32); B_rL = P.buf("rL")
                nb = P.sb(st, "bnb", [128, 2, 512], F32); B_nb = P.buf("nb")
                d_t = P.sb(st, "bdt", [128, 512], F32); B_d = P.buf("d")
                dsq = P.sb(st, "bdsq", [128, 512], BF16); B_dsq = P.buf("dsq")
                rs = P.sb(st, "brs", [128, 512], F32); B_rs = P.buf("rs")
                OTst = [P.sb(st, f"bOT{i}", [128, 512], BF16) for i in range(2)]; B_OT = [P.buf("OTst") for _ in range(2)]
                acc = [P.sb(st, f"bacc{i}", [128, 512], F32) for i in range(2)]
                B_acc = [P.buf("acc") for _ in range(2)]
                hi = P.sb(st, "bhi", [128, 512], BF16); B_hi = P.buf("hi")
                lo = P.sb(st, "blo", [128, 512], BF16); B_lo = P.buf("lo")

                def load_head(h):
                    sl = h % 2
                    if h >= PACK_FROM:
                        for m in range(2):
                            r0 = m * 512 + h * 64
                            sc.dma(lambda e, sl=sl, m=m, r0=r0: e.dma_start(out=KT[sl][0][m * 64:(m + 1) * 64, :], in_=KTs[r0:r0 + 64, :]),
                                   B_hd[sl], reads=[B_KTs], writes=[B_hd[sl]])
                            sc.dma(lambda e, sl=sl, m=m, r0=r0: e.dma_start(out=QT[sl][0][m * 64:(m + 1) * 64, :], in_=QTs[r0:r0 + 64, :]),
                                   B_hd[sl], reads=[B_QTs], writes=[B_hd[sl]])
                        sc.dma(lambda e, sl=sl, h=h: e.dma_start(out=btab[sl][:], in_=btab_d[h]),
                               B_hd[sl], reads=[B_in], writes=[B_hd[sl]])
                    else:
                        for m in range(2):
                            r0 = m * 512 + h * 64
                            sc.dma(lambda e, sl=sl, m=m, r0=r0: e.dma_start(out=KT[sl][m][0:64, :], in_=KTs[r0:r0 + 64, :]),
                                   B_hd[sl], reads=[B_KTs], writes=[B_hd[sl]])
                            sc.dma(lambda e, sl=sl, m=m, h=h: e.dma_start(out=KT[sl][m][64:69, :], in_=kconst[h]),
                                   B_hd[sl], reads=[B_in], writes=[B_hd[sl]])
                            sc.dma(lambda e, sl=sl, m=m, r0=r0: e.dma_start(out=QT[sl][m][0:64, :], in_=QTs[r0:r0 + 64, :]),
                                   B_hd[sl], reads=[B_QTs], writes=[B_hd[sl]])
                            sc.dma(lambda e, sl=sl, m=m, h=h: e.dma_start(out=QT[sl][m][64:69, :], in_=qconst[h]),
                                   B_hd[sl], reads=[B_in], writes=[B_hd[sl]])
                    vv = Vs[:, h * 128:(h + 1) * 128].rearrange("(j p) d -> p j d", p=128)
                    for j0 in range(0, NKB, 4):
                        sc.dma(lambda e, sl=sl, j0=j0, vv=vv: e.dma_start(out=VA[sl][:, j0:j0 + 4, :], in_=vv[:, j0:j0 + 4, :]),
                               B_hd[sl], reads=[B_Vs], writes=[B_hd[sl]])

                jobs = []
                for h in range(NH):
                    for i in range(NCH):
                        ab_ = (h * NCH + i) % 2
                        blocks = [("other", i2, jj) for i2 in range(i + 1) for jj in range(4)]
                        blocks += [("own", i2, jj) for i2 in range(i) for jj in range(4)]
                        blocks += [("diag", i, jj) for jj in range(4)]
                        for bi, (kind, i2, jj) in enumerate(blocks):
                            jobs.append(dict(h=h, sl=h % 2, i=i, bi=bi, nblk=len(blocks), kind=kind, ab=ab_,
                                             kcol=(NOWN if kind == "other" else 0) + i2 * 512 + jj * 128,
                                             q0=(jj * 128 if kind == "diag" else 0),
                                             omk=(kind == "other" and i2 == i)))

                def emit_qk(j, n):
                    sidx = n % 2
                    sl, kcol, q0, kind, omk, i, h = j["sl"], j["kcol"], j["q0"], j["kind"], j["omk"], j["i"], j["h"]
                    packed = h >= PACK_FROM

                    def mm_qk(e):
                        for m in range(2):
                            if packed:
                                ins = e.matmul(psum[:, 2 * sidx + m, q0:512], KT[sl][0][m * 64:(m + 1) * 64, kcol:kcol + 128],
                                               QT[sl][0][m * 64:(m + 1) * 64, i * 512 + q0:(i + 1) * 512],
                                               start=True, stop=not (kind == "diag"))
                            else:
                                ins = e.matmul(psum[:, 2 * sidx + m, q0:512], KT[sl][m][0:69, kcol:kcol + 128],
                                               QT[sl][m][0:69, i * 512 + q0:(i + 1) * 512],
                                               start=True, stop=not (kind == "diag" or omk))
                        for m in range(2):
                            if kind == "diag":
                                ins = e.matmul(psum[:, 2 * sidx + m, q0:q0 + 128], ident_sb[:], tri_sb[:],
                                               start=False, stop=True)
                            elif omk and not packed:
                                ins = e.matmul(psum[:, 2 * sidx + m, 0:512], ident_sb[:], omask_sb[:],
                                               start=False, stop=True)
                        return ins
                    sc.op("pe", mm_qk, reads=[B_hd[sl], B_cst], writes=[PS2[sidx]])

                def emit_exp(j, n):
                    sidx, pidx, q0 = n % 2, n % NPT, j["q0"]
                    if j["h"] >= PACK_FROM:
                        col = (j["kcol"] // 128) * NCH + j["i"]
                        sl = j["sl"]
                        sc.op("act", lambda e: e.activation(out=PT[pidx][:, :, q0:512], in_=psum[:, 2 * sidx:2 * sidx + 2, q0:512],
                                                            func=AF.Exp, bias=btab[sl][:, col:col + 1], scale=1.0),
                              reads=[PS2[sidx], B_hd[sl]], writes=[B_PT[pidx]])
                    else:
                        sc.op("act", lambda e: e.activation(out=PT[pidx][:, :, q0:512], in_=psum[:, 2 * sidx:2 * sidx + 2, q0:512],
                                                            func=AF.Exp), reads=[PS2[sidx]], writes=[B_PT[pidx]])

                def emit_pv(j, n):
                    pidx = n % NPT
                    sl, kcol, q0, bi, nblk = j["sl"], j["kcol"], j["q0"], j["bi"], j["nblk"]

                    def mm_pv(e):
                        for m in range(2):
                            e.matmul(psum[:, 4 + m, q0:512], VA[sl][:, kcol // 128, :], PT[pidx][:, m, q0:512],
                                     start=(bi == 0), stop=(bi == nblk - 1), skip_group_check=True)
                        return e.matmul(psum[:, 7, q0:512], ones1_sb[:], PT[pidx][:, 1, q0:512],
                                        start=(bi == 0), stop=(bi == nblk - 1), skip_group_check=True)
                    sc.op("pe", mm_pv, reads=[B_PT[pidx], B_hd[sl], B_cst], writes=[B_O, B_L1])
                    ab = j["ab"]
                    if bi == 0:
                        sc.op("dve", lambda e: e.tensor_copy(out=acc[ab][:], in_=PT[pidx][:, 0, :]),
                              reads=[B_PT[pidx]], writes=[B_acc[ab]])
                    else:
                        sc.op("dve", lambda e: e.tensor_tensor(out=acc[ab][:, q0:512], in0=acc[ab][:, q0:512],
                                                               in1=PT[pidx][:, 0, q0:512], op=ALU.add),
                              reads=[B_PT[pidx], B_acc[ab]], writes=[B_acc[ab]])

                ep = [0]

                def epilogue_stages(j):
                    h, i, ab = j["h"], j["i"], j["ab"]
                    o = ep[0] % 2
                    ep[0] += 1

                    def s0():
                        sc.op("act", lambda e: e.copy(out=Osb[:], in_=psum[:, 4:6, :]), reads=[B_O], writes=[B_Osb])
                        sc.op("dve", lambda e: e.tensor_copy(out=L1sb[:], in_=psum[:, 7, :]), reads=[B_L1], writes=[B_L1sb])
                        sc.op("dve", lambda e: e.tensor_copy(out=hi[:], in_=acc[ab][:]), reads=[B_acc[ab]], writes=[B_hi])
                        sc.op("dve", lambda e: e.tensor_tensor(out=lo[:], in0=acc[ab][:], in1=hi[:], op=ALU.subtract),
                              reads=[B_acc[ab], B_hi], writes=[B_lo])

                    def s1a():
                        def mm_L(e):
                            e.matmul(psum[:, 6, :], ones1_sb[:], hi[:], start=True, stop=False)
                            return e.matmul(psum[:, 6, :], ones1_sb[:], lo[:], start=False, stop=True)
                        sc.op("pe", mm_L, reads=[B_hi, B_lo, B_cst], writes=[B_L0])
                        sc.op("dve", lambda e: e.reciprocal(out=rL[:, 1, :], in_=L1sb[:]), reads=[B_L1sb, B_rL], writes=[B_rL])

                    def s1b():
                        sc.op("dve", lambda e: e.reciprocal(out=rL[:, 0, :], in_=psum[:, 6, :]), reads=[B_L0, B_rL], writes=[B_rL])

                    def s1():
                        sc.op("dve", lambda e: e.tensor_tensor(out=nb[:], in0=Osb[:], in1=rL[:], op=ALU.mult),
                              reads=[B_Osb, B_rL], writes=[B_nb])
                        sc.op("dve", lambda e: e.scalar_tensor_tensor(out=d_t[:], in0=nb[:, 1, :], scalar=neglam, in1=nb[:, 0, :],
                                                                      op0=ALU.mult, op1=ALU.add),
                              reads=[B_nb, B_lam], writes=[B_d])
                        sc.op("dve", lambda e: e.tensor_tensor(out=dsq[:], in0=d_t[:], in1=d_t[:], op=ALU.mult),
                              reads=[B_d], writes=[B_dsq])

                    def s2():
                        sc.op("pe", lambda e: e.matmul(psum[:, 6, :], ones128_sb[:], dsq[:], start=True, stop=True),
                              reads=[B_dsq, B_cst], writes=[B_L0])
                        sc.op("act", lambda e: e.activation(out=rs[:], in_=psum[:, 6, :], func=AF.Ln, bias=RMS_EPS, scale=1.0),
                              reads=[B_L0], writes=[B_rs])
                        sc.op("act", lambda e: e.activation(out=rs[:], in_=rs[:], func=AF.Exp, scale=-0.5),
                              reads=[B_rs], writes=[B_rs])

                    def s3():
                        sc.op("dve", lambda e: e.scalar_tensor_tensor(out=OTst[o][:], in0=d_t[:], scalar=gcolv, in1=rs[:],
                                                                      op0=ALU.mult, op1=ALU.mult),
                              reads=[B_d, B_rs, B_lam], writes=[B_OT[o]])
                        sc.dma(lambda e: e.dma_start(out=OTs[h * 128:(h + 1) * 128, i * 512:(i + 1) * 512], in_=OTst[o][:]),
                               B_OT[o], reads=[B_OT[o]], writes=[B_OTs])
                    return [s0, s1a, s1b, s1, s2, s3]

                load_head(0)
                emit_qk(jobs[0], 0)
                emit_qk(jobs[1], 1)
                pend = []
                for n, j in enumerate(jobs):
                    if j["bi"] == 0 and j["i"] == 0 and j["h"] + 1 < NH:
                        load_head(j["h"] + 1)
                    emit_exp(j, n)
                    if n + 2 < len(jobs):
                        emit_qk(jobs[n + 2], n + 2)
                    emit_pv(j, n)
                    if pend and j["bi"] in (0, 2, 4, 5, 6):
                        pend.pop(0)()
                    if j["bi"] == j["nblk"] - 1:
                        while pend:
                            pend.pop(0)()
                        stages = epilogue_stages(j)
                        stages[0]()
                        pend = stages[1:]
                while pend:
                    pend.pop(0)()
                sc.barrier()

        def phase_c1():
            T = 512
            with contextlib.ExitStack() as st:
                psum = st.enter_context(nc.psum_tensor("c1ps", [128, 8, 512], F32))
                PB = [P.buf(f"psum{i}") for i in range(8)]
                wgate = P.sb(st, "c1wg", [128, 8, 2048], BF16); B_wgt = P.buf("wgate")
                wpl = P.sb(st, "c1wpl", [128, 4, 128], BF16); B_wpl = P.buf("wpl")
                wpp = P.sb(st, "c1wpp", [128, 4, D], BF16); B_wpp = P.buf("wpp")
                wpa = P.sb(st, "c1wpa", [128, 8, D], BF16); B_wpa = P.buf("wpa")
                wo = P.sb(st, "c1wo", [128, 8, D], BF16); B_wo = P.buf("wo")
                xf = P.sb(st, "c1xf", [128, 8, T], F32); B_xf = [P.buf("xf") for _ in range(8)]
                xb0 = P.sb(st, "c1xb", [128, 8, T], BF16)
                ot0 = P.sb(st, "c1ot", [128, 8, T], BF16)
                U = P.sb(st, "c1U", [128, 4, 528], F32); B_U = P.buf("U")
                H1 = P.sb(st, "c1H1", [128, 4, 16], F32); H2 = P.sb(st, "c1H2", [128, 4, 16], F32); B_H = P.buf("H")
                icnt = P.sb(st, "c1icnt", [128, 4, T], F32); B_ic = P.buf("icnt")
                tA = [P.sb(st, f"c1tA{i}", [128, 528], F32) for i in range(2)]
                tB = [P.sb(st, f"c1tB{i}", [128, 528], F32) for i in range(2)]
                B_tw = [P.buf("tw") for _ in range(2)]
                zt = [P.sb(st, f"c1zt{g}", [128, T], BF16) for g in range(4)]; B_zt = [P.buf("zt") for _ in range(4)]
                pm = P.sb(st, "c1pm", [128, 4, T], BF16); B_pm = [P.buf("pm") for _ in range(4)]
                sA = [P.sb(st, f"c1sA{i}", [128, T], F32) for i in range(2)]; B_sA = [P.buf("sA") for _ in range(2)]
                sB = [P.sb(st, f"c1sB{i}", [128, T], F32) for i in range(2)]; B_sB = [P.buf("sB") for _ in range(2)]
                t1 = [P.sb(st, f"c1t1{i}", [128, T], F32) for i in range(2)]; B_t1 = [P.buf("t1") for _ in range(2)]
                t2 = [P.sb(st, f"c1t2{i}", [128, T], F32) for i in range(2)]; B_t2 = [P.buf("t2") for _ in range(2)]
                mix = P.sb(st, "c1mix", [128, 8, T], BF16); B_mix = [P.buf("mix") for _ in range(8)]
                sc.dma(lambda e: e.dma_start(out=icnt[:], in_=icnt_d[:]), B_ic, writes=[B_ic])

                def yprod(fo):
                    by = PB[fo % 2]
                    py = psum[:, fo % 2, :]

                    def mm_o(e, fo=fo, py=py):
                        for kt in range(8):
                            ins = e.matmul(py, wo[:, kt, fo * 128:(fo + 1) * 128], mix[:, kt, :], start=(kt == 0), stop=(kt == 7))
                        return ins
                    sc.op("pe", mm_o, reads=[B_wo] + B_mix, writes=[by])
                    return py, by
                ln_run = residual_ln("c1", st, psum, PB, None, None, T, yprod, 1.0 / ALPHA, 16, 24)
                stw = contextlib.ExitStack()
                stages = [(P.sb(stw, f"c1stg{i}", [128, 2048], F32), P.buf("stg")) for i in range(2)]
                P.load_weight(st, win, D, 2048, wgate, B_wgt, stages, col0=3584)
                sc.dma(lambda e: e.dma_start(out=stages[0][0][:, 0:512], in_=poolw_d.rearrange("p g d -> p (g d)")),
                       stages[0][1], writes=[stages[0][1]])
                for g in range(4):
                    sc.op("dve", lambda e, g=g: e.tensor_copy(out=wpl[:, g, :], in_=stages[0][0][:, g * 128:(g + 1) * 128]),
                          reads=[stages[0][1]], writes=[B_wpl])
                P.load_weight(st, wpp_d, 512, D, wpp, B_wpp, stages)
                P.load_weight(st, wpa_d, D, D, wpa, B_wpa, stages)
                P.load_weight(st, wout_d, D, D, wo, B_wo, stages)
                stw.close()
                xb1 = P.sb(st, "c1xb1", [128, 8, T], BF16)
                ot1 = P.sb(st, "c1ot1", [128, 8, T], BF16)
                xbs, ots = [xb0, xb1], [ot0, ot1]
                B_xbs = [P.buf("xb") for _ in range(2)]
                B_ots = [P.buf("ot") for _ in range(2)]
                B_xbs[1].r = dict(sc.alltok)
                B_ots[1].r = dict(sc.alltok)
                UTv = UTs.rearrange("(g p) t -> p g t", p=128)

                def load_xo(i):
                    s = i % 2
                    tk = slice(i * T, (i + 1) * T)
                    sc.dma(lambda e: e.dma_start(out=xbs[s][:], in_=x1bT.rearrange("(kt p) t -> p kt t", p=128)[:, :, tk]),
                           B_xbs[s], reads=[B_x1bT], writes=[B_xbs[s]])
                    sc.dma(lambda e: e.dma_start(out=ots[s][:], in_=OTs.rearrange("(kt p) t -> p kt t", p=128)[:, :, tk]),
                           B_ots[s], reads=[B_OTs], writes=[B_ots[s]])

                def load_u(i):
                    tk = slice(i * T, (i + 1) * T)
                    sc.dma(lambda e: e.dma_start(out=U[:, :, 16:528], in_=UTv[:, :, tk]), B_U, reads=[B_UTs], writes=[B_U])
                    if i >= 1:
                        c0 = NOWN + 512 * (i - 1) + 496
                        sc.dma(lambda e: e.dma_start(out=H1[:], in_=UTv[:, :, c0:c0 + 16]), B_H, reads=[B_UTs], writes=[B_H])
                    else:
                        sc.op("pool", lambda e: e.memset(H1[:], 0.0), writes=[B_H])
                    c1_ = NOWN + 512 * i + 496
                    sc.dma(lambda e: e.dma_start(out=H2[:], in_=UTv[:, :, c1_:c1_ + 16]), B_H, reads=[B_UTs], writes=[B_H])

                load_xo(0)
                load_u(0)
                c1pend = None
                for i in range(NCH):
                    tk = slice(i * T, (i + 1) * T)
                    xb, ot = xbs[i % 2], ots[i % 2]
                    B_xb, B_ot = B_xbs[i % 2], B_ots[i % 2]
                    def load_xf(tk=tk):
                        sc.dma(lambda e: e.dma_start(out=xf[:], in_=x1T.rearrange("(kt p) t -> p kt t", p=128)[:, :, tk]),
                               B_xf[0], reads=[B_x1T], writes=B_xf)
                    if i == 0:
                        load_xf()
                    if i + 1 < NCH:
                        load_xo(i + 1)
                    sc.op("dve", lambda e: e.tensor_scalar(out=U[:, :, 0:16], in0=H1[:], scalar1=cols_sb[:, 72:73], scalar2=None,
                                                          op0=ALU.mult), reads=[B_H, B_cols], writes=[B_U])
                    sc.op("dve", lambda e: e.scalar_tensor_tensor(out=U[:, :, 0:16], in0=H2[:], scalar=cols_sb[:, 73:74],
                                                                  in1=U[:, :, 0:16], op0=ALU.mult, op1=ALU.add),
                          reads=[B_H, B_cols, B_U], writes=[B_U])
                    for g in range(4):
                        eng = "dve"
                        ei = g % 2
                        A_, B_ = tA[ei], tB[ei]
                        Ug = U[:, g, :]
                        bw = B_tw[ei]
                        sc.op(eng, lambda e, A_=A_, Ug=Ug: e.tensor_tensor(out=A_[:, 1:528], in0=Ug[:, 1:528], in1=Ug[:, 0:527],
                                                                         op=ALU.add), reads=[B_U], writes=[bw])
                        cur = A_
                        if g >= 1:
                            sc.op(eng, lambda e, A_=A_, B_=B_: e.tensor_tensor(out=B_[:, 3:528], in0=A_[:, 3:528], in1=A_[:, 1:526],
                                                                             op=ALU.add), reads=[bw], writes=[bw])
                            cur = B_
                        if g >= 2:
                            sc.op(eng, lambda e, A_=A_, B_=B_: e.tensor_tensor(out=A_[:, 7:528], in0=B_[:, 7:528], in1=B_[:, 3:524],
                                                                             op=ALU.add), reads=[bw], writes=[bw])
                            cur = A_
                        if g >= 3:
                            sc.op(eng, lambda e, A_=A_, B_=B_: e.tensor_tensor(out=B_[:, 15:528], in0=A_[:, 15:528], in1=A_[:, 7:520],
                                                                             op=ALU.add), reads=[bw], writes=[bw])
                            cur = B_
                        wsz = float(2 ** (g + 1))
                        if i == 0:
                            other = B_ if cur is A_ else A_
                            sc.op(eng, lambda e, cur=cur, other=other, g=g: e.tensor_tensor(out=other[:, 16:528], in0=cur[:, 16:528],
                                                                                          in1=icnt[:, g, :], op=ALU.mult),
                                  reads=[bw, B_ic], writes=[bw])
                            sc.op(eng, lambda e, other=other, g=g, Ug=Ug: e.tensor_tensor(out=zt[g][:], in0=other[:, 16:528],
                                                                                        in1=Ug[:, 16:528], op=ALU.subtract),
                                  reads=[bw, B_U], writes=[B_zt[g]])
                        else:
                            sc.op("dve", lambda e, cur=cur, g=g, Ug=Ug, wsz=wsz: e.scalar_tensor_tensor(
                                out=zt[g][:], in0=cur[:, 16:528], scalar=1.0 / wsz, in1=Ug[:, 16:528],
                                op0=ALU.mult, op1=ALU.subtract), reads=[bw, B_U], writes=[B_zt[g]])
                        sc.op("pe", lambda e, g=g: e.matmul(psum[:, g, :], wpl[:, g, :], zt[g][:], start=True, stop=True),
                              reads=[B_wpl, B_zt[g]], writes=[PB[g]])
                        sc.op("dve", lambda e, g=g: e.tensor_scalar(out=pm[:, g, :], in0=psum[:, g, :],
                                                                   scalar1=cols_sb[:, 68 + g:69 + g], scalar2=bs_sb[:, g:g + 1],
                                                                   op0=ALU.mult, op1=ALU.add),
                              reads=[PB[g], B_cols], writes=[B_pm[g]])
                    if i + 1 < NCH:
                        load_u(i + 1)
                    for ft in range(8):
                        if c1pend is not None and ft >= 1:
                            for _ in range(3):
                                if c1pend[1]:
                                    c1pend[1].pop(0)()
                            if not c1pend[1]:
                                c1pend[2]()
                                c1pend = None
                                load_xf()
                        b0 = 4 * (ft % 2)
                        q = ft % 2
                        fs = slice(ft * 128, (ft + 1) * 128)

                        def mm_p0(e, fs=fs, b0=b0, ft=ft):
                            for kt in range(4):
                                ins = e.matmul(psum[:, b0, :], wpp[:, kt, fs], pm[:, kt, :], start=(kt == 0), stop=(kt == 3))
                            return ins

                        def mm_p(e, fs=fs, b0=b0, ft=ft, xb=xb, ot=ot):
                            for kt in range(8):
                                e.matmul(psum[:, b0 + 1, :], wpa[:, kt, fs], ot[:, kt, :], start=(kt == 0), stop=(kt == 7))
                            for kt in range(8):
                                e.matmul(psum[:, b0 + 2, :], wgate[:, kt, ft * 128:(ft + 1) * 128], xb[:, kt, :],
                                         start=(kt == 0), stop=(kt == 7))
                            for kt in range(8):
                                ins = e.matmul(psum[:, b0 + 3, :], wgate[:, kt, 1024 + ft * 128:1024 + (ft + 1) * 128], xb[:, kt, :],
                                               start=(kt == 0), stop=(kt == 7))
                            return ins
                        sc.op("pe", mm_p, reads=[B_wpa, B_wgt, B_xb, B_ot], writes=PB[b0 + 1:b0 + 4])
                        sc.op("pe", mm_p0, reads=[B_wpp] + B_pm, writes=[PB[b0]])
                        sc.op("act", lambda e, q=q, b0=b0, ft=ft: e.activation(out=sA[q][:], in_=psum[:, b0 + 2, :], func=AF.Sigmoid,
                                                                              bias=cols_sb[:, 48 + ft:49 + ft], scale=1.0),
                              reads=[PB[b0 + 2], B_cols], writes=[B_sA[q]])
                        sc.op("act", lambda e, q=q, b0=b0, ft=ft: e.activation(out=sB[q][:], in_=psum[:, b0 + 3, :], func=AF.Sigmoid,
                                                                              bias=cols_sb[:, 56 + ft:57 + ft], scale=1.0),
                              reads=[PB[b0 + 3], B_cols], writes=[B_sB[q]])
                        sc.op("dve", lambda e, q=q, b0=b0: e.tensor_tensor(out=t1[q][:], in0=sA[q][:], in1=psum[:, b0, :], op=ALU.mult),
                              reads=[B_sA[q], PB[b0]], writes=[B_t1[q]])
                        sc.op("dve", lambda e, q=q, b0=b0: e.tensor_tensor(out=t2[q][:], in0=sB[q][:], in1=psum[:, b0 + 1, :],
                                                                          op=ALU.mult),
                              reads=[B_sB[q], PB[b0 + 1]], writes=[B_t2[q]])
                        sc.op("dve", lambda e, q=q, ft=ft: e.tensor_tensor(out=mix[:, ft, :], in0=t1[q][:], in1=t2[q][:], op=ALU.add),
                              reads=[B_t1[q], B_t2[q]], writes=[B_mix[ft]])
                    steps = ln_run(xf, B_xf, None, None)

                    def store(tk=tk):
                        sc.dma(lambda e: e.dma_start(out=x2T.rearrange("(kt p) t -> p kt t", p=128)[:, :, tk], in_=xf[:]),
                               B_xf[0], reads=B_xf, writes=[B_x2T])
                    c1pend = (i, steps, store)
                while c1pend[1]:
                    c1pend[1].pop(0)()
                c1pend[2]()
                sc.barrier()

        if "A1" in phases:
            ffn_phase("a1", xT, B_in, w1g, w1u, w1d, 0, 8, NTOK // 256, x1T, B_x1T, NOWN // 256, x1bT, B_x1bT)
        if "A2" in phases:
            phase_a2()
        if "B" in phases:
            phase_b()
        if "C1" in phases:
            phase_c1()
        if "C2" in phases:
            ffn_phase("c2", x2T, B_x2T, w2g, w2u, w2d, 32, 40, NOWN // 256, outT, B_out, NOWN // 256, None, None)
        sc.barrier()
        blk = P.stack.enter_context(nc.Block())
        sc.replay(blk)
    return nc


def _colmajor(v, n):
    return np.ascontiguousarray(np.asarray(v, np.float32).reshape(n, 128).T)


def token_perm(r, nch):
    own = np.concatenate([np.arange(512 * (2 * i + r), 512 * (2 * i + r + 1)) for i in range(nch)])
    oth = np.concatenate([np.arange(512 * (2 * i + 1 - r), 512 * (2 * i + 2 - r)) for i in range(nch)])
    return own, oth


def make_inputs(inp, core, nch=8):
    b, r = core // 2, core % 2
    bf = ml_dtypes.bfloat16
    own, oth = token_perm(r, nch)
    pos = np.concatenate([own, oth])
    m = {}
    m["xT"] = np.ascontiguousarray(np.asarray(inp["x"][b], np.float32)[pos].T)
    cols = np.zeros((128, 80), np.float32)
    cols[:, 0:8] = _colmajor(inp["ln1_g"][0], 8)
    cols[:, 8:16] = _colmajor(inp["ln1_b"][0], 8)
    cols[:, 16:24] = _colmajor(inp["ln2_g"][0], 8)
    cols[:, 24:32] = _colmajor(inp["ln2_b"][0], 8)
    cols[:, 32:40] = _colmajor(inp["ln3_g"][0], 8)
    cols[:, 40:48] = _colmajor(inp["ln3_b"][0], 8)
    cols[:, 48:64] = _colmajor(inp["b_gate"][0], 16)
    cols[:, 64:68] = _colmajor(np.asarray(inp["pool_b"][0]).reshape(-1), 4)
    cols[:, 68:72] = _colmajor(inp["pool_scale"][0], 4)
    cols[:, 72] = 1.0 - r
    cols[:, 73] = float(r)
    m["cols"] = cols
    m["ones"] = np.full((128, 128), 1.0 / 1024.0, np.float32).astype(bf)
    m["ident"] = np.eye(128, dtype=np.float32).astype(bf)
    kk, qq = np.meshgrid(np.arange(128), np.arange(128), indexing="ij")
    m["tri"] = np.where(kk > qq, NEG, 0.0).astype(np.float32).astype(bf)
    m["omask"] = np.full((128, 512), NEG if r == 0 else 0.0, np.float32).astype(bf)
    slopes = 2.0 ** (-(np.arange(8) + 1.0))
    kc = np.zeros((8, 5, pos.size), np.float32)
    qc = np.zeros((8, 5, own.size), np.float32)
    for h in range(8):
        kc[h, 0] = slopes[h] * (pos % 128)
        kc[h, 1] = slopes[h] * 128.0 * (pos // 128)
        kc[h, 2:5] = 1.0
        ql = own % 512
        qc[h, 0:2] = 1.0
        qc[h, 2] = -slopes[h] * (ql & 255)
        qc[h, 3] = -slopes[h] * (ql & 256)
        qc[h, 4] = -slopes[h] * (own - ql)
    m["kconst"] = kc.astype(bf)
    nkb = pos.size // 128
    kpos = pos.reshape(nkb, 128).T.astype(np.float64)
    qend = own.reshape(nch, 512)[:, 511].astype(np.float64)
    bt = np.zeros((8, 128, nkb * nch), np.float32)
    blkmask = np.zeros((nkb, nch), np.float64)
    if r == 0:
        for i in range(nch):
            j0 = (own.size + 512 * i) // 128
            blkmask[j0:j0 + 4, i] = NEG
    for h in range(8):
        bt[h] = (slopes[h] * (kpos[:, :, None] - qend[None, None, :]) + blkmask[None]).reshape(128, nkb * nch)
    m["btab"] = bt
    m["qconst"] = qc.astype(bf)
    ic = np.zeros((128, 4, 512), np.float32)
    t0 = own[:512]
    for g in range(4):
        w = 2 ** (g + 1)
        ic[:, g, :] = (1.0 / np.minimum(t0 + 1, w))[None, :]
    m["icnt"] = ic
    lv = np.stack([inp["lambda_q1"][0], inp["lambda_k1"][0], inp["lambda_q2"][0], inp["lambda_k2"][0]]).astype(np.float32)
    m["lamv"] = np.ascontiguousarray(np.broadcast_to(lv[None], (128, 4, 64)))
    cols[:, 74] = np.asarray(inp["subln_g"][0], np.float32)
    m["ones1"] = np.ones((128, 128), np.float32).astype(bf)
    m["ones128"] = np.full((128, 128), 1.0 / 128.0, np.float32).astype(bf)
    m["w1g"] = np.asarray(inp["ffn1_w_gate"][0], np.float32)
    m["w1u"] = np.asarray(inp["ffn1_w_up"][0], np.float32)
    m["w1d"] = np.asarray(inp["ffn1_w_down"][0], np.float32)
    m["w2g"] = np.asarray(inp["ffn2_w_gate"][0], np.float32)
    m["w2u"] = np.asarray(inp["ffn2_w_up"][0], np.float32)
    m["w2d"] = np.asarray(inp["ffn2_w_down"][0], np.float32)
    m["win"] = np.asarray(inp["w_in"][0], np.float32)
    m["poolw"] = np.ascontiguousarray(np.asarray(inp["pool_w"][0], np.float32).transpose(1, 0, 2))
    m["wpp"] = np.asarray(inp["w_proj_pool"][0], np.float32)
    m["wpa"] = np.asarray(inp["w_proj_attn"][0], np.float32)
    m["wout"] = np.asarray(inp["w_out"][0], np.float32)
    return m


_NC_CACHE = {}


def kernel(**inputs):
    if "nc" not in _NC_CACHE:
        _NC_CACHE["nc"] = build({})
    nc = _NC_CACHE["nc"]
    in_maps = [make_inputs(inputs, c) for c in range(8)]
    res = run_bass_kernel_spmd(nc, in_maps, core_ids=list(range(8)))
    out = np.zeros((4, S, D), np.float32)
    for c in range(8):
        b, r = c // 2, c % 2
        own, _ = token_perm(r, 8)
        out[b, own, :] = np.asarray(res.results[c]["outT"], np.float32).T
    return out
```
